# Optimizing a Trainium2 kernel written in Bass

```python
import jax, jax.numpy as jnp
from jax import lax
import numpy as np

D_MODEL = 1024
BATCH = 16
SEQ = 256
DEPTH = 2
DEC_BATCH = 4
DEC_SEQ = 4096
PAST_LEN = 512

GRID_W = 64
N_EVEN = (DEPTH + 1) // 2
N_ODD = DEPTH // 2
D_FF = 4 * D_MODEL
EPS = 1e-6
SGU_CHUNK = 128
SGU_GROUPS = 4
SGU_WIDTH = D_MODEL // 2
SGU_GD = SGU_WIDTH // SGU_GROUPS
GLA_HEADS = 4
GLA_DV = (D_MODEL // 2) // GLA_HEADS
GLA_DK = GLA_DV // 2
GLA_RANK = 16
GLA_NORMALIZER = 16.0
GLA_CHUNK = 64
EV_SPLITS = (SGU_WIDTH,
             2 * SGU_WIDTH,
             2 * SGU_WIDTH + GLA_HEADS * GLA_DK,
             2 * SGU_WIDTH + 2 * GLA_HEADS * GLA_DK,
             2 * SGU_WIDTH + 2 * GLA_HEADS * GLA_DK + GLA_HEADS * GLA_DV,
             2 * SGU_WIDTH + 2 * GLA_HEADS * GLA_DK + 2 * GLA_HEADS * GLA_DV,
             2 * SGU_WIDTH + 2 * GLA_HEADS * GLA_DK + 2 * GLA_HEADS * GLA_DV + GLA_RANK)
EV_IN = 2 * SGU_WIDTH + 2 * GLA_HEADS * GLA_DK + 2 * GLA_HEADS * GLA_DV + 2 * GLA_RANK
EV_OUT = SGU_WIDTH + GLA_HEADS * GLA_DV
D_RNN = D_MODEL
RG_BLOCKS = 16
RG_BS = D_RNN // RG_BLOCKS
RG_CONV = 4
RG_PAD_L = 2
RG_PAD_R = 1
RG_C = 8.0

kernel_name = "hybrid_sgu_gla_rglru_diffusion_step"


def rmsnorm(x, g):
    xf = x.astype(jnp.float32)
    y = xf * lax.rsqrt(jnp.mean(xf * xf, axis=-1, keepdims=True) + EPS)
    return (y * g.astype(jnp.float32)).astype(x.dtype)


def layernorm(x, g, b):
    xf = x.astype(jnp.float32)
    mu = jnp.mean(xf, axis=-1, keepdims=True)
    xc = xf - mu
    y = xc * lax.rsqrt(jnp.mean(xc * xc, axis=-1, keepdims=True) + EPS)
    return (y * g.astype(jnp.float32) + b.astype(jnp.float32)).astype(x.dtype)


def grid_pos_embed(L, dtype):
    rows = L // GRID_W
    n = D_MODEL // 4
    omega = 1.0 / (10000.0 ** (jnp.arange(n, dtype=jnp.float32) / n))
    r = jnp.broadcast_to(jnp.arange(rows, dtype=jnp.float32)[:, None], (rows, GRID_W)).reshape(L)
    cc = jnp.broadcast_to(jnp.arange(GRID_W, dtype=jnp.float32)[None, :], (rows, GRID_W)).reshape(L)
    ar = r[:, None] * omega
    ac = cc[:, None] * omega
    return jnp.concatenate([jnp.sin(ar), jnp.cos(ar), jnp.sin(ac), jnp.cos(ac)], axis=-1).astype(dtype)


def gla_chunked(q, k, v, log_a, s0):
    B, H, L, DK = q.shape
    DV = v.shape[-1]
    n = L // GLA_CHUNK
    q, k, v, log_a = (t.reshape(B, H, n, GLA_CHUNK, t.shape[-1]) for t in (q, k, v, log_a))
    b = jnp.cumsum(log_a, axis=3)
    b_last = b[:, :, :, -1:, :]
    q_e = q * jnp.exp(b)
    k_e = k * jnp.exp(-b)
    k_d = k * jnp.exp(b_last - b)
    mask = jnp.tril(jnp.ones((GLA_CHUNK, GLA_CHUNK), dtype=bool))
    att = jnp.where(mask, jnp.einsum('bhnid,bhnjd->bhnij', q_e, k_e), 0.0)
    o_intra = jnp.einsum('bhnij,bhnjv->bhniv', att, v)
    ds = jnp.einsum('bhnjd,bhnjv->bhndv', k_d, v)
    decay = jnp.exp(b_last[:, :, :, 0, :])

    def step(s, inp):
        dec, d = inp
        return dec[..., None] * s + d, s

    s_final, s_in = lax.scan(step, s0, (jnp.moveaxis(decay, 2, 0), jnp.moveaxis(ds, 2, 0)))
    s_in = jnp.moveaxis(s_in, 0, 2)
    o_inter = jnp.einsum('bhnid,bhndv->bhniv', q_e, s_in)
    return (o_intra + o_inter).reshape(B, H, L, DV), s_final


def linear_scan(a, b, h0, reverse):
    idx = -1 if reverse else 0
    b = b.at[:, idx].add(a[:, idx] * h0)

    def combine(left, right):
        a_l, b_l = left
        a_r, b_r = right
        return a_l * a_r, a_r * b_l + b_r

    _, hs = lax.associative_scan(combine, (a, b), reverse=reverse, axis=1)
    final = hs[:, 0] if reverse else hs[:, -1]
    return hs, final


def even_mixer(hin, s0, w_in, w_out, ln_g, ln_b, ws, bs, gw2, gb, gnorm):
    B, L, _ = hin.shape
    u, v, q, k, vv, g, lr_f, lr_b = jnp.split(hin @ w_in, EV_SPLITS, axis=-1)
    u = jax.nn.gelu(u)
    v = layernorm(jax.nn.gelu(v), ln_g, ln_b)
    n = L // SGU_CHUNK
    v = v.reshape(B, n, SGU_CHUNK, SGU_GROUPS, SGU_GD)
    sv = jnp.einsum('gpq,bnqgd->bnpgd', ws, v) + bs.T[None, None, :, :, None]
    out_a = u * sv.reshape(B, L, SGU_WIDTH)
    def heads(t):
        return t.reshape(B, L, GLA_HEADS, -1).transpose(0, 2, 1, 3).astype(jnp.float32)
    qh = heads(q) * (GLA_DK ** -0.5)
    kh = heads(k)
    vh = heads(vv)
    la_f = jax.nn.log_sigmoid(heads(lr_f @ gw2[0] + gb[0])) / GLA_NORMALIZER
    la_b = jax.nn.log_sigmoid(heads(lr_b @ gw2[1] + gb[1])) / GLA_NORMALIZER
    s0 = s0.astype(jnp.float32)
    o_f, sf = gla_chunked(qh, kh, vh, la_f, s0[:, 0])
    rev = lambda t: jnp.flip(t, axis=2)
    o_b, sb = gla_chunked(rev(qh), rev(kh), rev(vh), rev(la_b), s0[:, 1])
    o = o_f + rev(o_b)
    o = o * lax.rsqrt(jnp.mean(o * o, axis=-1, keepdims=True) + EPS) \
        * gnorm.reshape(GLA_HEADS, 1, GLA_DV).astype(jnp.float32)
    o = o.transpose(0, 2, 1, 3).reshape(B, L, GLA_HEADS * GLA_DV).astype(hin.dtype) * jax.nn.silu(g)
    y = jnp.concatenate([out_a, o], axis=-1) @ w_out
    return y, jnp.stack([sf, sb], axis=1)


def odd_mixer(hin, s0, w_in, conv_w, conv_b, wa, ba, wx, bx, lam, w_out):
    B, L, _ = hin.shape
    xb, gbr = jnp.split(hin @ w_in, 2, axis=-1)
    xc = lax.conv_general_dilated(xb, conv_w[:, None, :].astype(xb.dtype), (1,), [(RG_PAD_L, RG_PAD_R)],
                                  dimension_numbers=('NWC', 'WIO', 'NWC'),
                                  feature_group_count=D_RNN) + conv_b
    xf = xc.astype(jnp.float32)
    xblk = xf.reshape(B, L, RG_BLOCKS, RG_BS)
    s0 = s0.astype(jnp.float32)

    def direction(d, reverse):
        r = jax.nn.sigmoid(jnp.einsum('blhi,hij->blhj', xblk, wa[d]).reshape(B, L, D_RNN) + ba[d])
        i = jax.nn.sigmoid(jnp.einsum('blhi,hij->blhj', xblk, wx[d]).reshape(B, L, D_RNN) + bx[d])
        log_a = -RG_C * r * jax.nn.softplus(-lam[d].astype(jnp.float32))
        a = jnp.exp(log_a)
        bt = jnp.sqrt(-jnp.expm1(2.0 * log_a)) * (i * xf)
        return linear_scan(a, bt, s0[:, d], reverse)

    h_f, sf = direction(0, False)
    h_b, sb = direction(1, True)
    y = ((h_f + h_b).astype(hin.dtype) * jax.nn.gelu(gbr)) @ w_out
    return y, jnp.stack([sf, sb], axis=1)


def setup_inputs(seed: int = 0) -> dict:
    key = jax.random.key(seed)
    ks = jax.random.split(key, 32)

    def nrm(k, shape, scale):
        return jax.random.normal(k, shape, jnp.float32) * scale

    D = D_MODEL
    rg_u = jax.random.uniform(ks[29], (N_ODD, 2, D_RNN), jnp.float32, 0.9, 0.999)
    rg_p = rg_u ** (1.0 / RG_C)
    return {
        'x_prompt': nrm(ks[0], (BATCH, SEQ, D), 1.0),
        'x_sample': nrm(ks[1], (DEC_BATCH, DEC_SEQ, D), 1.0),
        'c': nrm(ks[2], (DEC_BATCH, D), 1.0),
        'state_gla': nrm(ks[3], (DEC_BATCH, N_EVEN, 2, GLA_HEADS, GLA_DK, GLA_DV), 1.0),
        'state_rglru': nrm(ks[4], (DEC_BATCH, N_ODD, 2, D_RNN), 1.0),
        'c_ctx': nrm(ks[5], (D,), 1.0),
        'mod_w': nrm(ks[6], (DEPTH, D, 6 * D), D ** -0.5),
        'mod_b': nrm(ks[7], (DEPTH, 6 * D), 0.02),
        'norm_g': 1.0 + nrm(ks[8], (DEPTH, 4, D), 0.02),
        'mlp_w1': nrm(ks[9], (DEPTH, D, D_FF), D ** -0.5),
        'mlp_b1': nrm(ks[10], (DEPTH, D_FF), 0.02),
        'mlp_w2': nrm(ks[11], (DEPTH, D_FF, D), D_FF ** -0.5),
        'mlp_b2': nrm(ks[12], (DEPTH, D), 0.02),
        'ev_w_in': nrm(ks[13], (N_EVEN, D, EV_IN), D ** -0.5),
        'ev_w_out': nrm(ks[14], (N_EVEN, EV_OUT, D), EV_OUT ** -0.5),
        'sgu_ln_g': 1.0 + nrm(ks[15], (N_EVEN, SGU_WIDTH), 0.02),
        'sgu_ln_b': nrm(ks[16], (N_EVEN, SGU_WIDTH), 0.02),
        'sgu_ws': nrm(ks[17], (N_EVEN, SGU_GROUPS, SGU_CHUNK, SGU_CHUNK), SGU_CHUNK ** -0.5),
        'sgu_bs': nrm(ks[18], (N_EVEN, SGU_GROUPS, SGU_CHUNK), 0.02),
        'gla_gate_w2': nrm(ks[19], (N_EVEN, 2, GLA_RANK, GLA_HEADS * GLA_DK), GLA_RANK ** -0.5),
        'gla_gate_b': nrm(ks[20], (N_EVEN, 2, GLA_HEADS * GLA_DK), 0.02),
        'gla_norm_g': 1.0 + nrm(ks[21], (N_EVEN, GLA_HEADS * GLA_DV), 0.02),
        'rg_w_in': nrm(ks[22], (N_ODD, D, 2 * D_RNN), D ** -0.5),
        'rg_conv_w': nrm(ks[23], (N_ODD, RG_CONV, D_RNN), RG_CONV ** -0.5),
        'rg_conv_b': nrm(ks[24], (N_ODD, D_RNN), 0.02),
        'rg_wa': nrm(ks[25], (N_ODD, 2, RG_BLOCKS, RG_BS, RG_BS), RG_BS ** -0.5),
        'rg_ba': nrm(ks[26], (N_ODD, 2, D_RNN), 0.02),
        'rg_wx': nrm(ks[27], (N_ODD, 2, RG_BLOCKS, RG_BS, RG_BS), RG_BS ** -0.5),
        'rg_bx': nrm(ks[28], (N_ODD, 2, D_RNN), 0.02),
        'rg_L': jnp.log(rg_p) - jnp.log1p(-rg_p),
        'rg_w_out': nrm(ks[30], (N_ODD, D_RNN, D), D_RNN ** -0.5),
    }


def reference(x_prompt, x_sample, c, state_gla, state_rglru, c_ctx, mod_w, mod_b, norm_g,
              mlp_w1, mlp_b1, mlp_w2, mlp_b2, ev_w_in, ev_w_out, sgu_ln_g, sgu_ln_b, sgu_ws, sgu_bs,
              gla_gate_w2, gla_gate_b, gla_norm_g, rg_w_in, rg_conv_w, rg_conv_b, rg_wa, rg_ba,
              rg_wx, rg_bx, rg_L, rg_w_out):
    def trunk(x, cond, gla_init, rg_init):
        gla_fin = []
        rg_fin = []
        for i in range(DEPTH):
            mod = (jax.nn.silu(cond) @ mod_w[i] + mod_b[i])[:, None, :]
            sh1, sc1, g1, sh2, sc2, g2 = jnp.split(mod, 6, axis=-1)
            hmix = rmsnorm(x, norm_g[i, 0]) * (1.0 + sc1) + sh1
            if i % 2 == 0:
                e = i // 2
                y, s = even_mixer(hmix, gla_init[:, e], ev_w_in[e], ev_w_out[e], sgu_ln_g[e], sgu_ln_b[e],
                                  sgu_ws[e], sgu_bs[e], gla_gate_w2[e], gla_gate_b[e], gla_norm_g[e])
                gla_fin.append(s)
            else:
                o = i // 2
                y, s = odd_mixer(hmix, rg_init[:, o], rg_w_in[o], rg_conv_w[o], rg_conv_b[o], rg_wa[o],
                                 rg_ba[o], rg_wx[o], rg_bx[o], rg_L[o], rg_w_out[o])
                rg_fin.append(s)
            x = x + g1 * rmsnorm(y, norm_g[i, 1])
            hff = rmsnorm(x, norm_g[i, 2]) * (1.0 + sc2) + sh2
            f = jnp.square(jax.nn.relu(hff @ mlp_w1[i] + mlp_b1[i])) @ mlp_w2[i] + mlp_b2[i]
            x = x + g2 * rmsnorm(f, norm_g[i, 3])
        return x, jnp.stack(gla_fin, axis=1), jnp.stack(rg_fin, axis=1)

    nb = x_prompt.shape[0]
    gla_zero = jnp.zeros((nb, N_EVEN, 2, GLA_HEADS, GLA_DK, GLA_DV), jnp.float32)
    rg_zero = jnp.zeros((nb, N_ODD, 2, D_RNN), jnp.float32)
    y_prompt, new_state_gla, new_state_rglru = trunk(x_prompt, c_ctx[None, :], gla_zero, rg_zero)
    xs = x_sample + grid_pos_embed(x_sample.shape[1], x_sample.dtype)[None]
    y_sample, _, _ = trunk(xs, c, state_gla, state_rglru)
    return (y_prompt, y_sample, new_state_gla, new_state_rglru)
```

```python
import numpy as np
import concourse.bass as bass
import concourse.mybir as mybir
from concourse.bass_utils import run_bass_kernel_spmd

F32 = mybir.dt.float32
BF16 = mybir.dt.bfloat16
I32 = mybir.dt.int32
AF = mybir.ActivationFunctionType
ALU = mybir.AluOpType

NCORE = 8
D = 1024
NTOK = 2560
NTILE = 20
NMB = 10
NSB = 5
EPS = 1e-6
EV_IN = 2592
TWO_PI = 6.283185307179586
SELF_WAIT = True


class Buf:
    __slots__ = ("name", "w", "r")

    def __init__(self, name=""):
        self.name = name
        self.w = {}
        self.r = {}


class Stream:
    def __init__(self, name, sem):
        self.name = name
        self.sem = sem
        self.ops = []
        self.cnt = 0
        self.seen = {}


class DSem:
    def __init__(self, key, h):
        self.key = key
        self.h = h
        self.val = 0


class Prog:
    def __init__(self, sems, dsem_handles):
        self.st = {n: Stream(n, sems.get(n)) for n in ("pe", "act", "dve", "pool", "sp")}
        self.free_dsems = [DSem("d%d" % i, h) for i, h in enumerate(dsem_handles)]
        self.dsems = []
        self.nops = 0
        self.stopped = False
        self.nself = {}

    def new_dsem(self):
        d = self.free_dsems.pop()
        self.dsems.append(d)
        return d

    def _wait(self, st, tok):
        key, h, val = tok
        if st.seen.get(key, 0) >= val:
            return
        st.seen[key] = val
        st.ops.append(lambda e: e.wait_ge(h, val))

    def _deps(self, reads, writes):
        toks = []
        for b in reads:
            toks.extend(b.w.values())
        for b in writes:
            toks.extend(b.w.values())
            toks.extend(b.r.values())
        return toks

    def _mark(self, tok, reads, writes):
        for b in reads:
            b.r[tok[0]] = tok
        for b in writes:
            b.w = {tok[0]: tok}
            b.r = {}

    def op(self, eng, fn, reads=(), writes=()):
        if self.stopped:
            return None
        st = self.st[eng]
        for t in self._deps(reads, writes):
            if t[0] == eng and not SELF_WAIT:
                continue
            if t[0] == eng and st.seen.get(eng, 0) < t[2]:
                self.nself[eng] = self.nself.get(eng, 0) + 1
            self._wait(st, t)
        st.cnt += 1
        sem = st.sem
        tok = (eng, sem, st.cnt)
        st.ops.append(lambda e: fn(e).then_inc(sem, 1))
        self._mark(tok, reads, writes)
        self.nops += 1
        return tok

    def pe(self, fns, reads=(), writes=()):
        if self.stopped:
            return None
        st = self.st["pe"]
        for t in self._deps(reads, writes):
            if t[0] == "pe":
                continue
            self._wait(st, t)
        st.cnt += 1
        sem = st.sem
        tok = ("pe", sem, st.cnt)
        for f in fns[:-1]:
            st.ops.append(f)
        last = fns[-1]
        st.ops.append(lambda e: last(e).then_inc(sem, 1))
        self._mark(tok, reads, writes)
        self.nops += len(fns)
        return tok

    def dma(self, q, fn, dsem, reads=(), writes=()):
        if self.stopped:
            return None
        st = self.st[q]
        toks = self._deps(reads, writes)
        if dsem.val > 0:
            toks.append((dsem.key, dsem.h, dsem.val))
        for t in toks:
            self._wait(st, t)
        dsem.val += 16
        h = dsem.h
        tok = (dsem.key, h, dsem.val)
        st.ops.append(lambda e: fn(e).then_inc(h, 16))
        self._mark(tok, reads, writes)
        self.nops += 1
        return tok

    def barrier(self):
        if self.stopped:
            return
        toks = [(n, s.sem, s.cnt) for n, s in self.st.items() if s.cnt > 0]
        toks += [(d.key, d.h, d.val) for d in self.dsems if d.val > 0]
        for st in self.st.values():
            for t in toks:
                if t[0] != st.name:
                    self._wait(st, t)


class Arena:
    def __init__(self, t, n):
        self.t = t
        self.n = n
        self.off = 0

    def reset(self):
        self.off = 0

    def f32(self, n):
        assert self.off + n <= self.n, ("arena overflow", self.off, n, self.n)
        ap = self.t[:, self.off:self.off + n]
        self.off += n
        return ap

    def bf16(self, n):
        n32 = (n + 1) // 2
        return self.f32(n32).bitcast(BF16)[:, 0:n]


def act_fn(out, in_, func, bias=None, scale=None, accum_out=None):
    kw = {}
    if bias is not None:
        kw["bias"] = bias
    if scale is not None:
        kw["scale"] = scale
    if accum_out is not None:
        kw["accum_out"] = accum_out
    return lambda e: e.activation(out=out, in_=in_, func=func, **kw)


def mm(out, lhsT, rhs, start, stop):
    return lambda e: e.matmul(out, lhsT, rhs, start=start, stop=stop)


def tt(out, in0, in1, op):
    return lambda e: e.tensor_tensor(out=out, in0=in0, in1=in1, op=op)


def ts(out, in0, s1, s2, op0, op1=None):
    if op1 is None:
        return lambda e: e.tensor_scalar(out=out, in0=in0, scalar1=s1, scalar2=None, op0=op0)
    return lambda e: e.tensor_scalar(out=out, in0=in0, scalar1=s1, scalar2=s2, op0=op0, op1=op1)


def stt(out, in0, scalar, in1, op0, op1):
    return lambda e: e.scalar_tensor_tensor(out=out, in0=in0, scalar=scalar, in1=in1, op0=op0, op1=op1)


def cp(out, in_):
    return lambda e: e.tensor_copy(out=out, in_=in_)


class _Stop(Exception):
    pass


def build_program(debug_phase=None, stop_after=None):
    nc = bass.Bass("TRN2", target_bir_lowering=False)

    def din(name, shape, dt=F32):
        return nc.dram_tensor(name, list(shape), dt, kind="ExternalInput")

    def dout(name, shape, dt=F32):
        return nc.dram_tensor(name, list(shape), dt, kind="ExternalOutput")

    xtok = din("xtok", [NTOK, D])
    prow = din("prow", [3, 128, 128])
    flags_d = din("flags", [128, 4])
    ginit_d = din("ginit", [2, 4, 64, 128])
    mod_w = din("mod_w", [2, D, 6 * D])
    mlp_w1 = din("mlp_w1", [2, D, 4 * D])
    mlp_w2 = din("mlp_w2", [2, 4 * D, D])
    ev_w_in = din("ev_w_in", [D, EV_IN])
    ev_w_out = din("ev_w_out", [D, D])
    lng_d = din("sgu_ln_g", [1, 512])
    lnb_d = din("sgu_ln_b", [1, 512])
    ws_d = din("sgu_ws", [4, 128, 128])
    bs_d = din("sgu_bs", [1, 512])
    gw2_d = din("gw2", [2, 16, 256])
    gb_d = din("gb", [1, 512])
    rg_w_in = din("rg_w_in", [D, 2 * D])
    rg_wa = din("rg_wa", [2, 16, 64, 64])
    rg_wx = din("rg_wx", [2, 16, 64, 64])
    rg_w_out = din("rg_w_out", [D, D])
    rgb_d = din("rgb", [1, 4096])

    y_d = dout("y", [NTOK, D])
    nsg_d = dout("nsg", [2, 2, 4, 64, 128])
    nsr_d = dout("nsr", [2, 2, D])
    dbg_d = dout("dbg", [128, 8, NTOK]) if debug_phase is not None else None

    cc1_in = nc.dram_tensor("cc1_in", [256, 256], F32)
    cc1_out = nc.dram_tensor("cc1_out", [512, 256], F32)
    cc2_in = nc.dram_tensor("cc2_in", [128, 24], F32)
    cc2_out = nc.dram_tensor("cc2_out", [256, 24], F32)
    cc3_in = nc.dram_tensor("cc3_in", [128, 16], F32)
    cc3_out = nc.dram_tensor("cc3_out", [256, 16], F32)
    PAIRS = [[0, 1], [2, 3], [4, 5], [6, 7]]
    wcache = nc.dram_tensor("wcache", [16, 128, 4096], BF16)
    wocache = nc.dram_tensor("wocache", [4, 128, 2048], BF16)

    import contextlib
    es = contextlib.ExitStack()
    with es:
        def sb(name, shape, dt=F32):
            return es.enter_context(nc.sbuf_tensor(name, list(shape), dt))

        XT = sb("XT", [128, 8, NTOK])
        NWB = 24832
        WBt = sb("WB", [128, NWB], BF16)
        NSC = 15296
        SCt = sb("SC", [128, NSC])
        CST = sb("CST", [128, 5, 128])
        NEGC = sb("NEGC", [128, 2])
        CBF = sb("CBF", [128, 3, 128], BF16)
        PCOL = sb("PCOL", [128, 3, 128])
        MODV = sb("MODV", [128, 2, 6, 8, 2])
        DER = sb("DER", [128, 2, 6, 2, 8])
        LNG = sb("LNG", [128, 512])
        LNB = sb("LNB", [128, 512])
        GWBD = sb("GWBD", [32, 512], BF16)
        GBROW = sb("GBROW", [1, 512], BF16)
        BSROW = sb("BSROW", [1, 512], BF16)
        WST = sb("WST", [128, 4, 128], BF16)
        GINIT = sb("GINIT", [128, 2, 2, 128])
        FLG = sb("FLG", [128, 4])
        SCOND = sb("SCOND", [128, 8, 2])
        SCONDB = sb("SCONDB", [128, 8, 2], BF16)
        CLV = sb("CLV", [128, 2, 2, 8])
        PS = es.enter_context(nc.psum_tensor("ps", [128, 4096], F32))
        psum = [PS[:, i * 512:(i + 1) * 512] for i in range(8)]
        PSB = [Buf("ps%d" % i) for i in range(8)]

        eng_sems = {n: es.enter_context(nc.semaphore("s_" + n)) for n in ("pe", "act", "dve", "pool")}
        dsem_h = [es.enter_context(nc.semaphore("dq%d" % i)) for i in range(90)]
        cc_sem = es.enter_context(nc.semaphore("cc"))
        P = Prog(eng_sems, dsem_h)
        SC = Arena(SCt, NSC)

        IDENT = CST[:, 0, :]
        TRI_IF = CST[:, 1, :]
        TRI_IB = CST[:, 2, :]
        TRI_SF = CST[:, 3, :]
        TRI_SB = CST[:, 4, :]
        ONESB = CBF[:, 0, :]
        MASKB16 = [CBF[:, 1, :], CBF[:, 2, :]]
        B_CST = Buf("cst")
        B_PCOL = Buf("pcol")
        B_MISC = Buf("misc")
        B_MOD = Buf("mod")
        B_DER = Buf("der")
        XBUF = [Buf("x%d" % i) for i in range(NTILE)]

        def pcol(g, row, n=1):
            return PCOL[:, g, row:row + n]

        ds_misc = P.new_dsem()
        SC.reset()
        IOT_I = SC.f32(128).bitcast(I32)
        IOT_F = SC.f32(128)
        KI = SC.f32(2).bitcast(I32)
        KF = SC.f32(2)
        OMEGA = SC.f32(2)
        RI = SC.f32(64).bitcast(I32)
        RF = SC.f32(64)
        ANG = SC.f32(4 * 64).rearrange("p (a b) -> p a b", a=4)
        ANGK = SC.f32(4 * 64).rearrange("p (a b) -> p a b", a=4)
        ANGKI = SC.f32(4 * 64).bitcast(I32).rearrange("p (a b) -> p a b", a=4)
        POS = SC.f32(64)
        PEROW = SC.f32(4 * 32).rearrange("p (a b) -> p a b", a=4)
        PECOL = SC.f32(4 * 64).rearrange("p (a b) -> p a b", a=4)
        PROW = SC.f32(3 * 128).rearrange("p (a b) -> p a b", a=3)
        WSS = SC.f32(4 * 128).rearrange("p (a b) -> p a b", a=4)
        B_T = Buf("setup_tmp")
        B_PROW = Buf("prow")
        B_WSS = Buf("wss")

        P.dma("sp", lambda e: e.dma_start(out=PROW, in_=prow.ap().rearrange("g r c -> r g c")), ds_misc, writes=[B_PROW])
        d2 = P.new_dsem()
        P.dma("sp", lambda e: e.dma_start(out=FLG[:], in_=flags_d.ap()), d2, writes=[B_MISC])
        d3 = P.new_dsem()
        P.dma("sp", lambda e: e.dma_start(out=LNG[:], in_=lng_d.ap().to_broadcast([128, 512])), d3, writes=[B_MISC])
        d4 = P.new_dsem()
        P.dma("sp", lambda e: e.dma_start(out=LNB[:], in_=lnb_d.ap().to_broadcast([128, 512])), d4, writes=[B_MISC])
        d5 = P.new_dsem()
        P.dma("sp", lambda e: e.dma_start(out=WSS, in_=ws_d.ap().rearrange("g p q -> p g q")), d5, writes=[B_WSS])
        d6 = P.new_dsem()
        P.dma("sp", lambda e: e.dma_start(
            out=GINIT[:], in_=ginit_d.ap().rearrange("r (hp hh) d v -> (hh d) r hp v", hh=2)), d6, writes=[B_MISC])
        P.op("pool", lambda e: e.memset(GWBD[:], 0.0), writes=[B_MISC])
        d7 = P.new_dsem()
        P.dma("pool", lambda e: e.dma_start(out=GWBD[0:16, 0:256], in_=gw2_d.ap()[0]), d7, writes=[B_MISC])
        d8 = P.new_dsem()
        P.dma("pool", lambda e: e.dma_start(out=GWBD[16:32, 256:512], in_=gw2_d.ap()[1]), d8, writes=[B_MISC])
        d9 = P.new_dsem()
        P.dma("pool", lambda e: e.dma_start(out=GBROW[:], in_=gb_d.ap()), d9, writes=[B_MISC])
        d10 = P.new_dsem()
        P.dma("pool", lambda e: e.dma_start(out=BSROW[:], in_=bs_d.ap()), d10, writes=[B_MISC])

        P.op("pool", lambda e: e.iota(IOT_I, pattern=[[1, 128]], base=0, channel_multiplier=-1), writes=[B_T])
        P.op("dve", cp(IOT_F, IOT_I), reads=[B_T], writes=[B_T])
        P.op("dve", ts(IDENT, IOT_F, 0.0, None, ALU.is_equal), reads=[B_T], writes=[B_CST])
        P.op("dve", ts(TRI_IF, IOT_F, 0.0, -1.0 / 16, ALU.is_ge, ALU.mult), reads=[B_T], writes=[B_CST])
        P.op("dve", ts(TRI_IB, IOT_F, 0.0, -1.0 / 16, ALU.is_le, ALU.mult), reads=[B_T], writes=[B_CST])
        P.op("dve", ts(TRI_SF, IOT_F, 0.0, -1.0 / 16, ALU.is_lt, ALU.mult), reads=[B_T], writes=[B_CST])
        P.op("dve", ts(TRI_SB, IOT_F, 0.0, -1.0 / 16, ALU.is_gt, ALU.mult), reads=[B_T], writes=[B_CST])
        P.op("dve", ts(MASKB16[0], IOT_F, 0.0, None, ALU.is_ge), reads=[B_T], writes=[B_CST])
        P.op("dve", ts(MASKB16[1], IOT_F, 0.0, None, ALU.is_le), reads=[B_T], writes=[B_CST])
        P.op("dve", lambda e: e.memset(ONESB, 1.0), writes=[B_CST])
        P.op("dve", lambda e: e.memset(NEGC[:], -1.0 / 16), writes=[B_CST])

        for g in range(3):
            P.pe([lambda e, g=g: e.transpose(psum[0][:, g * 128:(g + 1) * 128], PROW[:, g, :], IDENT)],
                 reads=[B_PROW, B_CST], writes=[PSB[0]])
        P.op("dve", cp(PCOL[:].rearrange("p a b -> p (a b)"), psum[0][:, 0:384]), reads=[PSB[0]], writes=[B_PCOL])
        for g in range(4):
            P.pe([lambda e, g=g: e.transpose(psum[1][:, g * 128:(g + 1) * 128], WSS[:, g, :], IDENT)],
                 reads=[B_WSS, B_CST], writes=[PSB[1]])
        P.op("dve", cp(WST[:].rearrange("p a b -> p (a b)"), psum[1][:, 0:512]), reads=[PSB[1]], writes=[B_MISC])
        P.op("act", act_fn(SCOND[:].rearrange("p c q -> p q c"),
                           PCOL[:, 0, 112:128].rearrange("p (q c) -> p q c", q=2), AF.Silu),
             reads=[B_PCOL], writes=[B_MISC])
        P.op("dve", cp(SCONDB[:], SCOND[:]), reads=[B_MISC], writes=[B_MISC])
        CLT = SC.f32(16)
        P.op("act", act_fn(CLT, PCOL[:, 2, 76:92], AF.Exp, scale=-1.0), reads=[B_PCOL], writes=[B_T])
        P.op("act", act_fn(CLT, CLT, AF.Ln, bias=1.0), reads=[B_T], writes=[B_T])
        P.op("dve", ts(CLV[:, 0].rearrange("p a b -> p (a b)"), CLT, -8.0, None, ALU.mult), reads=[B_T], writes=[B_MISC])
        P.op("dve", ts(CLV[:, 1].rearrange("p a b -> p (a b)"), CLT, -16.0, None, ALU.mult), reads=[B_T], writes=[B_MISC])

        P.op("pool", lambda e: e.iota(KI, pattern=[[128, 2]], base=0, channel_multiplier=1), writes=[B_T])
        P.op("dve", cp(KF, KI), reads=[B_T], writes=[B_T])
        P.op("act", act_fn(OMEGA, KF, AF.Exp, scale=-float(np.log(10000.0)) / 256.0), reads=[B_T], writes=[B_T])
        P.op("pool", lambda e: e.iota(RI, pattern=[[1, 64]], base=0, channel_multiplier=0), writes=[B_T])
        P.op("dve", cp(RF, RI), reads=[B_T], writes=[B_T])

        def sincos_table(dst, n, rowoff):
            if rowoff:
                P.op("dve", ts(POS[:, 0:n], RF[:, 0:n], FLG[:, 2:3], None, ALU.add), reads=[B_T, B_MISC], writes=[B_T])
            else:
                P.op("dve", cp(POS[:, 0:n], RF[:, 0:n]), reads=[B_T], writes=[B_T])
            for c2 in range(2):
                P.op("dve", ts(ANG[:, c2, 0:n], POS[:, 0:n], OMEGA[:, c2:c2 + 1], None, ALU.mult), reads=[B_T], writes=[B_T])
                P.op("dve", ts(ANG[:, 2 + c2, 0:n], POS[:, 0:n], OMEGA[:, c2:c2 + 1], 0.5 * np.pi, ALU.mult, ALU.add),
                     reads=[B_T], writes=[B_T])
            P.op("dve", ts(ANGKI[:, :, 0:n], ANG[:, :, 0:n], 1.0 / TWO_PI, None, ALU.mult), reads=[B_T], writes=[B_T])
            P.op("dve", cp(ANGK[:, :, 0:n], ANGKI[:, :, 0:n]), reads=[B_T], writes=[B_T])
            P.op("dve", stt(ANG[:, :, 0:n], ANGK[:, :, 0:n], -TWO_PI, ANG[:, :, 0:n], ALU.mult, ALU.add),
                 reads=[B_T], writes=[B_T])
            P.op("dve", ts(ANG[:, :, 0:n], ANG[:, :, 0:n], 3.14159, -3.14159, ALU.min, ALU.max), reads=[B_T], writes=[B_T])
            P.op("act", act_fn(dst, ANG[:, :, 0:n], AF.Sin), reads=[B_T], writes=[B_MISC])

        sincos_table(PECOL[:], 64, False)
        sincos_table(PEROW[:], 32, True)

        def wslot(off, kc, ncol):
            return WBt[:, off:off + kc * ncol].rearrange("p (k n) -> p k n", k=kc)

        W_IN0 = wslot(0, 8, EV_IN)
        WIN0_B = [Buf("win0_%d" % i) for i in range(6)]
        WIN0_S = [P.new_dsem() for _ in range(6)]
        WOUT_OFF = 8 * EV_IN
        W_OUTS = wslot(WOUT_OFF, 8, 512)
        WOUT_B = Buf("wouts")
        WOUT_S = P.new_dsem()

        def load_w_rows(dst, src_ap, c0, ncol, dsem, buf, q="pool"):
            P.dma(q, lambda e: e.dma_start(out=dst, in_=src_ap.rearrange("(k p) n -> p k n", p=128)[:, :, c0:c0 + ncol]),
                  dsem, writes=[buf])

        class WStream:
            def __init__(self, off, src):
                self.slots = [WBt[:, off + i * 2048:off + (i + 1) * 2048].rearrange("p (k n) -> p k n", k=8) for i in range(2)]
                self.bufs = [Buf("wq0"), Buf("wq1")]
                self.sems = [P.new_dsem(), P.new_dsem()]
                self.hsems = [P.new_dsem(), P.new_dsem()]
                self.csems = [P.new_dsem(), P.new_dsem()]
                self.cbufs = [Buf("woc%d" % i) for i in range(4)]
                self.src = src
                self.issued = 0
                self.base = 0

            def load(self):
                qi = self.issued
                self.issued += 1
                qt, sl = qi % 4, qi % 2
                flat = self.slots[sl].rearrange("p k n -> p (k n)")
                if qi < 4:
                    load_w_rows(self.slots[sl], self.src.ap(), qt * 256, 256, self.sems[sl], self.bufs[sl])
                    P.dma("sp", lambda e: e.dma_start(out=wocache.ap()[qt], in_=flat), self.csems[sl],
                          reads=[self.bufs[sl]], writes=[self.cbufs[qt]])
                else:
                    P.dma("sp", lambda e: e.dma_start(out=flat, in_=wocache.ap()[qt]), self.hsems[sl],
                          reads=[self.cbufs[qt]], writes=[self.bufs[sl]])

            def prefetch(self):
                while self.issued < self.base + 2:
                    self.load()

            def project(self, cat, b_cat, n, out_ap, out_buf, last):
                for q_ in range(4):
                    while self.issued < min(self.base + q_ + 2, self.base + 4):
                        self.load()
                    sl = (self.base + q_) % 2
                    for o2 in range(2):
                        oc = q_ * 2 + o2
                        P.pe([mm(out_ap(oc), self.slots[sl][:, k, o2 * 128:(o2 + 1) * 128], cat[:, k, 0:n], k == 0, k == 7)
                              for k in range(8)], reads=[b_cat, self.bufs[sl]], writes=[out_buf(oc)])
                self.base += 4
                if not last:
                    self.prefetch()

        def win0_bufs(lo, hi):
            return [WIN0_B[i] for i in range(lo // 512, (hi - 1) // 512 + 1)]

        def load_win0():
            for i in (2, 3, 4, 5, 0, 1):
                c0 = i * 512
                ncol = min(512, EV_IN - c0)
                load_w_rows(W_IN0[:, :, c0:c0 + ncol], ev_w_in.ap(), c0, ncol, WIN0_S[i], WIN0_B[i])

        NMW = 4
        MW = [SC.bf16(8 * 512).rearrange("p (k n) -> p k n", k=8) for _ in range(NMW)]
        MW_B = [Buf("mw%d" % i) for i in range(NMW)]
        MW_S = [P.new_dsem() for _ in range(NMW)]
        STG = [SC.f32(1024) for _ in range(2)]
        STG_B = [Buf("stg%d" % i) for i in range(2)]
        STG_S = [P.new_dsem() for _ in range(2)]
        assert SC.off <= NSC

        def mod_piece(l, pc, mw, mw_b, mw_s, pb):
            P.dma("pool", lambda e: e.dma_start(
                out=mw, in_=mod_w.ap()[l].rearrange("(k p) n -> p k n", p=128)[:, :, pc * 512:(pc + 1) * 512]),
                mw_s, writes=[mw_b])
            fns = []
            for cc in range(4):
                for k in range(8):
                    fns.append(mm(psum[pb][:, 256 + cc * 2:256 + cc * 2 + 2], mw[:, k, cc * 128:(cc + 1) * 128], SCONDB[:, k, :],
                                  k == 0, k == 7))
            P.pe(fns, reads=[mw_b, B_MISC], writes=[PSB[pb]])
            for cc in range(4):
                col = pc * 4 + cc
                j, c = col // 8, col % 8
                P.op("dve", ts(MODV[:, l, j, c, :], psum[pb][:, 256 + cc * 2:256 + cc * 2 + 2], pcol(0, l * 48 + col), None, ALU.add),
                     reads=[PSB[pb], B_PCOL], writes=[B_MOD])

        def derive(l):
            for q in range(2):
                mv = lambda j: MODV[:, l, j, :, q]
                ng = lambda j: PCOL[:, 1, l * 32 + j * 8:l * 32 + j * 8 + 8]
                P.op("dve", stt(DER[:, l, 0, q, :], mv(1), 1.0, ng(0), ALU.add, ALU.mult), reads=[B_MOD, B_PCOL], writes=[B_DER])
                P.op("dve", cp(DER[:, l, 1, q, :], mv(0)), reads=[B_MOD], writes=[B_DER])
                P.op("dve", tt(DER[:, l, 2, q, :], mv(2), ng(1), ALU.mult), reads=[B_MOD, B_PCOL], writes=[B_DER])
                P.op("dve", stt(DER[:, l, 3, q, :], mv(4), 1.0, ng(2), ALU.add, ALU.mult), reads=[B_MOD, B_PCOL], writes=[B_DER])
                P.op("dve", cp(DER[:, l, 4, q, :], mv(3)), reads=[B_MOD], writes=[B_DER])
                P.op("dve", tt(DER[:, l, 5, q, :], mv(5), ng(3), ALU.mult), reads=[B_MOD, B_PCOL], writes=[B_DER])

        def load_tile(t):
            s = t % 2
            P.dma("sp", lambda e: e.dma_start(out=STG[s], in_=xtok.ap()[t * 128:(t + 1) * 128, :]), STG_S[s], writes=[STG_B[s]])
            for half in range(2):
                pb = 3 + half
                fns = [lambda e, c=c, pb=pb, half=half: e.transpose(
                    psum[pb][:, (c - 4 * half) * 128:(c - 4 * half + 1) * 128], STG[s][:, c * 128:(c + 1) * 128], IDENT)
                    for c in range(4 * half, 4 * half + 4)]
                P.pe(fns, reads=[STG_B[s], B_CST], writes=[PSB[pb]])
            xs = XT[:, :, t * 128:(t + 1) * 128]
            if t < 4:
                for half in range(2):
                    P.op("dve" if half == 0 else "act",
                         (cp(xs[:, 4 * half:4 * half + 4, :], psum[3 + half][:].rearrange("p (c n) -> p c n", c=4))
                          if half == 0 else
                          act_fn(xs[:, 4 * half:4 * half + 4, :], psum[3 + half][:].rearrange("p (c n) -> p c n", c=4), AF.Copy)),
                         reads=[PSB[3 + half]], writes=[XBUF[t]])
            else:
                r0 = (t - 4) * 2
                for c in range(4):
                    for rr in range(2):
                        P.op("dve", ts(xs[:, c, rr * 64:(rr + 1) * 64], psum[3][:, c * 128 + rr * 64:c * 128 + rr * 64 + 64],
                                       PEROW[:, c, r0 + rr:r0 + rr + 1], None, ALU.add),
                             reads=[PSB[3], B_MISC], writes=[XBUF[t]])
                for c in range(4):
                    P.op("dve", tt(xs[:, 4 + c, :].rearrange("p (r n) -> p r n", r=2),
                                   psum[4][:, c * 128:(c + 1) * 128].rearrange("p (r n) -> p r n", r=2),
                                   PECOL[:, c, :].unsqueeze(1).to_broadcast([128, 2, 64]), ALU.add),
                         reads=[PSB[4], B_MISC], writes=[XBUF[t]])

        for pc in range(12):
            mod_piece(0, pc, MW[pc % NMW], MW_B[pc % NMW], MW_S[pc % NMW], 2)
        derive(0)
        load_win0()
        for t in range(NTILE):
            load_tile(t)

        def xcols(c0, n):
            return XT[:, :, c0:c0 + n]

        def xbufs(c0, n):
            return XBUF[c0 // 128:(c0 + n - 1) // 128 + 1]

        MODE = {"lnexp": False}

        def rstd_from(ps_ap, n, scale, tmp, tmp_b, rstd, rstd_b, ps_b):
            if MODE["lnexp"]:
                P.op("act", act_fn(tmp[:, 0:n], ps_ap, AF.Ln, bias=EPS, scale=scale), reads=[ps_b], writes=[tmp_b])
                P.op("act", act_fn(rstd[:, 0:n], tmp[:, 0:n], AF.Exp, scale=-0.5), reads=[tmp_b], writes=[rstd_b])
            else:
                P.op("act", act_fn(tmp[:, 0:n], ps_ap, AF.Sqrt, bias=EPS, scale=scale), reads=[ps_b], writes=[tmp_b])
                P.op("dve", lambda e: e.reciprocal(out=rstd[:, 0:n], in_=tmp[:, 0:n]), reads=[tmp_b], writes=[rstd_b])

        def norm_mod(src, src_bufs, n, l, which, q, HM, HM_B, SQ, SQ_B, RSTD, RSTD_B, TMPS, TMP_BS, pb):
            ja, jb = (0, 1) if which == 1 else (3, 4)
            P.op("act", act_fn(SQ[:, :, 0:n], src, AF.Square), reads=src_bufs, writes=[SQ_B])
            P.pe([mm(psum[pb][:, 0:n], ONESB, SQ[:, c, 0:n], c == 0, c == 7) for c in range(8)],
                 reads=[SQ_B, B_CST], writes=[PSB[pb]])
            rstd_from(psum[pb][:, 0:n], n, 1.0 / D, TMPS[0], TMP_BS[0], RSTD, RSTD_B, PSB[pb])
            for c in range(8):
                k = c % 2
                P.op("dve", tt(TMPS[k][:, 0:n], src[:, c, :], RSTD[:, 0:n], ALU.mult),
                     reads=list(src_bufs) + [RSTD_B], writes=[TMP_BS[k]])
                if c % 2 == 0:
                    P.op("act", act_fn(HM[:, c, 0:n], TMPS[k][:, 0:n], AF.Identity,
                                       bias=DER[:, l, jb, q, c:c + 1], scale=DER[:, l, ja, q, c:c + 1]),
                         reads=[TMP_BS[k], B_DER], writes=[HM_B])
                else:
                    P.op("dve", ts(HM[:, c, 0:n], TMPS[k][:, 0:n], DER[:, l, ja, q, c:c + 1], DER[:, l, jb, q, c:c + 1],
                                   ALU.mult, ALU.add), reads=[TMP_BS[k], B_DER], writes=[HM_B])

        def post_update(ychunks, y_bufs, c0, n, l, which, q, SQ, SQ_B, RSTD, RSTD_B, TMPS, TMP_BS, pb, sq_done=False, y_all=None):
            jg = 2 if which == 1 else 5
            if y_all is not None:
                P.op("act", act_fn(SQ[:, :, 0:n], y_all, AF.Square), reads=y_bufs, writes=[SQ_B])
            elif not sq_done:
                for c in range(8):
                    P.op("act", act_fn(SQ[:, c, 0:n], ychunks[c], AF.Square), reads=y_bufs, writes=[SQ_B])
            P.pe([mm(psum[pb][:, 0:n], ONESB, SQ[:, c, 0:n], c == 0, c == 7) for c in range(8)],
                 reads=[SQ_B, B_CST], writes=[PSB[pb]])
            rstd_from(psum[pb][:, 0:n], n, 1.0 / D, TMPS[0], TMP_BS[0], RSTD, RSTD_B, PSB[pb])
            xb = xbufs(c0, n)

            def upd(c):
                k = c % 2
                P.op("dve", stt(XT[:, c, c0:c0 + n], TMPS[k][:, 0:n], DER[:, l, jg, q, c:c + 1], XT[:, c, c0:c0 + n],
                                ALU.mult, ALU.add),
                     reads=[TMP_BS[k], B_DER] + xb, writes=xb)
            for c in range(8):
                k = c % 2
                P.op("dve", tt(TMPS[k][:, 0:n], ychunks[c], RSTD[:, 0:n], ALU.mult),
                     reads=list(y_bufs) + [RSTD_B], writes=[TMP_BS[k]])
                if c > 0:
                    upd(c - 1)
            upd(7)

        def dump_x(tag):
            if debug_phase == tag:
                dd = P.new_dsem()
                P.dma("sp", lambda e: e.dma_start(out=dbg_d.ap(), in_=XT[:]), dd, reads=XBUF)
            if stop_after == tag and not P.stopped:
                P.barrier()
                P.stopped = True

        P.barrier()
        dump_x("load")

        SC.reset()
        MODE["lnexp"] = True
        HMS = [SC.bf16(8 * 128).rearrange("p (c n) -> p c n", c=8) for _ in range(2)]
        HM = HMS[0]
        SQ = HM
        RSTD = SC.f32(128)
        TMPS = [SC.f32(128), SC.f32(128)]
        U = SC.bf16(4 * 128).rearrange("p (c n) -> p c n", c=4)
        SG = SC.bf16(4 * 128).rearrange("p (c n) -> p c n", c=4)
        QF = SC.f32(2 * 128).rearrange("p (c n) -> p c n", c=2)
        KF_ = SC.f32(2 * 128).rearrange("p (c n) -> p c n", c=2)
        LR = SC.bf16(128)
        CAT = SC.bf16(8 * 128).rearrange("p (c n) -> p c n", c=8)
        VG = SC.f32(512)
        VTM = SC.bf16(512)
        VVTM = SC.bf16(512)
        KTM = SC.f32(256)
        LA = SC.f32(512)
        ED = SC.f32(256)
        KD = SC.bf16(256)
        EB = SC.f32(512)
        ENB = SC.f32(512)
        QEZ = SC.bf16(1024).rearrange("p (r h n) -> p r h n", r=2, h=4)
        KE = SC.bf16(512).rearrange("p (r h n) -> p r h n", r=2, h=2)
        P2REG = SC.f32(2048)
        ATT = P2REG[:, 0:512].bitcast(BF16).rearrange("p (r h n) -> p r h n", r=2, h=4)
        SIN16 = P2REG[:, 512:768].bitcast(BF16).rearrange("p (r h n) -> p r h n", r=2, h=2)
        OSQ = P2REG[:, 768:1024].bitcast(BF16)
        ORS = P2REG[:, 1024:1536]
        OT = P2REG[:, 1536:2048]
        MW1 = P2REG.bitcast(BF16).rearrange("p (k n) -> p k n", k=8)
        MW1_B, MW1_S = Buf("mw1"), P.new_dsem()
        DECF = SC.f32(4)
        LNST = SC.f32(8)
        LNMV = SC.f32(4)
        DSB = SC.bf16(NTILE * 256).rearrange("p (t h n) -> p t h n", t=NTILE, h=2)
        GST = SC.f32(7 * 256).rearrange("p (s h n) -> p s h n", s=7, h=2)
        NSG = SC.f32(8 * 128).rearrange("p (s r h n) -> p s r h n", s=2, r=2, h=2)
        GDEC = SC.f32(NTILE * 2).rearrange("p (t h) -> p t h", h=2)
        GPB = SC.f32(NTILE * 2).rearrange("p (t h) -> p t h", h=2)
        assert SC.off <= NSC, SC.off
        print("L0 scratch cols", SC.off, "of", NSC)
        (B_HM, B_RSTD, B_U, B_SG, B_QF, B_KF, B_LR, B_CAT, B_VG, B_VTM, B_VVTM, B_KTM, B_LA, B_ED,
         B_KD, B_EB, B_ENB, B_QE, B_KE, B_ATT, B_SIN, B_OSQ, B_ORS, B_OT, B_DECF, B_LN) = [Buf("l0_%d" % i) for i in range(26)]
        B_HMS = [B_HM, Buf("l0_hm1")]
        B_SQ = B_HM
        B_TMPS = [Buf("tmp0"), Buf("tmp1")]
        DSB_B = [Buf("dsb%d" % t) for t in range(NTILE)]
        B_GD = Buf("gdec")
        SF, RB0, RB1, RECVF, RECVB, SFI = (GST[:, i] for i in range(6))
        B_SF, B_SFI, B_RECV, B_NSGF, B_NSGB = Buf("sf"), Buf("sfi"), Buf("recv"), Buf("nsgf"), Buf("nsgb")

        def cond_of_tile(t):
            return 0 if t < 4 else 1

        def l0_block_norm(t):
            c0 = t * 128
            hm, b_hm = HMS[t % 2], B_HMS[t % 2]
            norm_mod(xcols(c0, 128), xbufs(c0, 128), 128, 0, 1, cond_of_tile(t), hm, b_hm, hm, b_hm, RSTD, B_RSTD,
                     TMPS, B_TMPS, 0)

        def fm_proj(col0, m, pb, n=128):
            P.pe([mm(psum[pb][0:m, 0:n], W_IN0[:, k, col0:col0 + m], HM[:, k, 0:n], k == 0, k == 7)
                  for k in range(8)], reads=[B_HM] + win0_bufs(col0, col0 + m), writes=[PSB[pb]])

        def l0_lr():
            fm_proj(2560, 32, 1)
            P.op("act", act_fn(LR[0:32, :], psum[1][0:32, 0:128], AF.Copy), reads=[PSB[1]], writes=[B_LR])

        def l0_tile_tm(want_v):
            P.pe([mm(psum[2][:, 0:256], HM[:, k, :], W_IN0[:, k, 1280:1536], k == 0, k == 7) for k in range(8)],
                 reads=[B_HM] + win0_bufs(1280, 1536), writes=[PSB[2]])
            P.pe([mm(psum[3][:, 0:512], HM[:, k, :], W_IN0[:, k, 1536:2048], k == 0, k == 7) for k in range(8)],
                 reads=[B_HM] + win0_bufs(1536, 2048), writes=[PSB[3]])
            P.op("act", act_fn(KTM, psum[2][:, 0:256], AF.Copy), reads=[PSB[2]], writes=[B_KTM])
            P.op("dve", cp(VVTM, psum[3][:, 0:512]), reads=[PSB[3]], writes=[B_VVTM])
            P.pe([mm(psum[4][:, 0:512], LR[0:32, :], GWBD[:], True, False),
                  mm(psum[4][:, 0:512], ONESB[0:1, :], GBROW[:], False, True)],
                 reads=[B_LR, B_MISC, B_CST], writes=[PSB[4]])
            P.op("act", act_fn(LA, psum[4][:, 0:512], AF.Exp, scale=-1.0), reads=[PSB[4]], writes=[B_LA])
            P.op("act", act_fn(LA, LA, AF.Ln, bias=1.0), reads=[B_LA], writes=[B_LA])
            if want_v:
                P.pe([mm(psum[5][:, 0:512], HM[:, k, :], W_IN0[:, k, 512:1024], k == 0, k == 7) for k in range(8)],
                     reads=[B_HM] + win0_bufs(512, 1024), writes=[PSB[5]])

        def l0_kd_ds(r, pb_e, pb_ds):
            tri = TRI_SF if r == 0 else TRI_SB
            P.pe([mm(psum[pb_e][:, 0:256], tri, LA[:, r * 256:(r + 1) * 256], True, True)],
                 reads=[B_LA, B_CST], writes=[PSB[pb_e]])
            P.op("act", act_fn(ED, psum[pb_e][:, 0:256], AF.Exp), reads=[PSB[pb_e]], writes=[B_ED])
            P.op("dve", tt(KD, KTM, ED, ALU.mult), reads=[B_KTM, B_ED], writes=[B_KD])
            fns = []
            for h in range(4):
                hp, hh = h // 2, h % 2
                fns.append(mm(psum[pb_ds][hh * 64:(hh + 1) * 64, hp * 128:(hp + 1) * 128], KD[:, h * 64:(h + 1) * 64],
                              VVTM[:, h * 128:(h + 1) * 128], True, True))
            P.pe(fns, reads=[B_KD, B_VVTM], writes=[PSB[pb_ds]])

        def l0_dec(r, dst, dst_b, pb):
            fns = [mm(psum[pb][:, 256 + 2 * hp:256 + 2 * hp + 2], LA[:, r * 256 + hp * 128:r * 256 + (hp + 1) * 128], NEGC[:, 0:2], True, True)
                   for hp in range(2)]
            P.pe(fns, reads=[B_LA, B_CST], writes=[PSB[pb]])
            P.op("act", act_fn(dst, psum[pb][:, 256:260].rearrange("p (h two) -> p h two", two=2)[:, :, 0], AF.Exp),
                 reads=[PSB[pb]], writes=[dst_b])

        def gla_state_step(S, S_B, dec, dec_b, ds_ap, ds_b):
            for hp in range(2):
                P.op("dve", stt(S[:, hp, :], S[:, hp, :], dec[:, hp:hp + 1], ds_ap[:, hp * 128:(hp + 1) * 128], ALU.mult, ALU.add),
                     reads=[S_B, dec_b, ds_b], writes=[S_B])

        dns_f = P.new_dsem()
        dns_b = P.new_dsem()
        l0_block_norm(0)
        for t in range(NTILE):
            HM, B_HM = HMS[t % 2], B_HMS[t % 2]
            l0_lr()
            if t == 2 or t == 0:
                P.op("dve", lambda e: e.memset(SF[:], 0.0), writes=[B_SF])
            if t == 4:
                P.op("dve", cp(SF[:], GINIT[:, 0]), reads=[B_MISC], writes=[B_SF])
            l0_tile_tm(False)
            if t + 1 < NTILE:
                l0_block_norm(t + 1)
            l0_kd_ds(0, 5, 6)
            l0_dec(0, DECF[:, 0:2], B_DECF, 5)
            gla_state_step(SF, B_SF, DECF, B_DECF, psum[6][:, 0:256], PSB[6])
            if t in (1, 3):
                P.op("act", act_fn(NSG[:, t // 2, 0], SF[:], AF.Copy), reads=[B_SF], writes=[B_NSGF])
            l0_kd_ds(1, 5, 7)
            l0_dec(1, GDEC[:, t, :], B_GD, 5)
            P.op("act", act_fn(DSB[:, t].rearrange("p h n -> p (h n)"), psum[7][:, 0:256], AF.Copy),
                 reads=[PSB[7]], writes=[DSB_B[t]])
            if t < 12:
                mod_piece(1, t, MW1, MW1_B, MW1_S, 1)
        derive(1)
        P.barrier()
        dump_x("l0p1")
        RBS = [RB0, RB1]
        B_RBS = [Buf("rb0"), Buf("rb1")]
        B_GPB = Buf("gpb")
        PRUN = [GST[:, 6, 0, 0:2], GST[:, 6, 1, 0:2]]
        B_PRUN = [Buf("prun0"), Buf("prun1")]

        def bwd_scan(tiles, init_ap, track_p):
            cur = 0
            if init_ap is None:
                P.op("dve", lambda e: e.memset(RBS[0][:], 0.0), writes=[B_RBS[0]])
            else:
                P.op("dve", cp(RBS[0][:], init_ap), reads=[B_MISC], writes=[B_RBS[0]])
            if track_p:
                P.op("dve", lambda e: e.memset(PRUN[0], 1.0), writes=[B_PRUN[0]])
            pc_ = 0
            for t in reversed(tiles):
                nxt = 1 - cur
                for hp in range(2):
                    P.op("dve", stt(RBS[nxt][:, hp, :], RBS[cur][:, hp, :], GDEC[:, t, hp:hp + 1], DSB[:, t, hp, :], ALU.mult, ALU.add),
                         reads=[B_RBS[cur], B_GD, DSB_B[t]], writes=[B_RBS[nxt]])
                P.op("act", act_fn(DSB[:, t], RBS[cur][:], AF.Copy), reads=[B_RBS[cur]], writes=[DSB_B[t]])
                if track_p:
                    P.op("dve", cp(GPB[:, t, :], PRUN[pc_]), reads=[B_PRUN[pc_]], writes=[B_GPB])
                    P.op("dve", tt(PRUN[1 - pc_], PRUN[pc_], GDEC[:, t, :], ALU.mult), reads=[B_PRUN[pc_], B_GD], writes=[B_PRUN[1 - pc_]])
                    pc_ = 1 - pc_
                cur = nxt
            return cur

        for s in range(2):
            cur = bwd_scan([2 * s, 2 * s + 1], None, False)
            P.op("act", act_fn(NSG[:, s, 1], RBS[cur][:], AF.Copy), reads=[B_RBS[cur]], writes=[B_NSGB])
        cur = bwd_scan(list(range(4, NTILE)), GINIT[:, 1], True)
        dcc = P.new_dsem()
        B_CC1a, B_CC1b = Buf("cc1a"), Buf("cc1b")
        P.dma("sp", lambda e: e.dma_start(out=cc1_in.ap()[0:128, :], in_=SF[:].rearrange("p h n -> p (h n)")), dcc, reads=[B_SF], writes=[B_CC1a])
        dcc2 = P.new_dsem()
        P.dma("sp", lambda e: e.dma_start(out=cc1_in.ap()[128:256, :], in_=RBS[cur][:].rearrange("p h n -> p (h n)")), dcc2,
              reads=[B_RBS[cur]], writes=[B_CC1b])
        cc_count = [0]

        def exchange(src, dst, b_ins, b_out):
            if P.stopped:
                return
            st = P.st["pool"]
            for tkn in P._deps(b_ins, [b_out]):
                P._wait(st, tkn)
            cc_count[0] += 1
            v = cc_count[0]
            st.ops.append(lambda e: e.collective_compute("AllGather", ALU.bypass, replica_groups=PAIRS,
                                                         ins=[src.ap().opt()], outs=[dst.ap().opt()]).then_inc(cc_sem, 1))
            tok = ("cc", cc_sem, v)
            P._mark(tok, b_ins, [b_out])

        B_CC1O = Buf("cc1o")
        exchange(cc1_in, cc1_out, [B_CC1a, B_CC1b], B_CC1O)
        dcc3 = P.new_dsem()
        B_RECVF, B_RECVB = Buf("recvf"), Buf("recvb")
        P.dma("sp", lambda e: e.dma_start(out=RECVF[:].rearrange("p h n -> p (h n)"), in_=cc1_out.ap()[0:128, :]), dcc3,
              reads=[B_CC1O], writes=[B_RECVF])
        dcc4 = P.new_dsem()
        P.dma("sp", lambda e: e.dma_start(out=RECVB[:].rearrange("p h n -> p (h n)"), in_=cc1_out.ap()[384:512, :]), dcc4,
              reads=[B_CC1O], writes=[B_RECVB])
        P.op("dve", stt(SFI[:], RECVF[:], FLG[:, 1:2], GINIT[:, 0], ALU.mult, ALU.add), reads=[B_RECVF, B_MISC], writes=[B_SFI])
        P.op("dve", ts(RECVB[:], RECVB[:], FLG[:, 0:1], None, ALU.mult), reads=[B_RECVB, B_MISC], writes=[B_RECVB])
        for t in range(4, NTILE):
            for hp in range(2):
                P.op("dve", stt(DSB[:, t, hp, :], RECVB[:, hp, :], GPB[:, t, hp:hp + 1], DSB[:, t, hp, :], ALU.mult, ALU.add),
                     reads=[B_RECVB, B_GPB, DSB_B[t]], writes=[DSB_B[t]])
        P.dma("sp", lambda e: e.dma_start(out=nsg_d.ap().rearrange("s r (hp hh) d v -> (hh d) s r hp v", hh=2), in_=NSG[:]),
              dns_f, reads=[B_NSGF, B_NSGB])

        P.barrier()
        dump_x("l0ex")

        def out_proj_and_update(src, t, l, q):
            WS0.project(CAT, B_CAT, 128, lambda oc: psum[oc // 4][:, (oc % 4) * 128:(oc % 4) * 128 + 128],
                        lambda oc: PSB[oc // 4], t == NTILE - 1)
            ych = [psum[c // 4][:, (c % 4) * 128:(c % 4) * 128 + 128] for c in range(8)]
            post_update(ych, PSB[0:2], t * 128, 128, l, 1, q, SQ, B_SQ, RSTD, B_RSTD, TMPS, B_TMPS, 4,
                        y_all=PS[:, 0:1024].rearrange("p (c n) -> p c n", c=8))

        WS0 = WStream(WOUT_OFF, ev_w_out)
        WS0.prefetch()
        P.op("dve", lambda e: e.memset(QEZ[:], 0.0), writes=[B_QE])
        HGN = GST[:, 6, 1, 8:12]
        B_HGN = Buf("hgn")
        P.op("dve", ts(HGN, PCOL[:, 2, 0:4], 0.5, None, ALU.mult), reads=[B_PCOL], writes=[B_HGN])
        l0_block_norm(0)
        for t in range(NTILE):
            q = cond_of_tile(t)
            HM, B_HM = HMS[t % 2], B_HMS[t % 2]
            SQ, B_SQ = HM, B_HM
            l0_lr()
            if t in (0, 2):
                P.op("dve", lambda e: e.memset(SF[:], 0.0), writes=[B_SF])
            if t == 4:
                P.op("dve", cp(SF[:], SFI[:]), reads=[B_SFI], writes=[B_SF])
            l0_tile_tm(True)
            for cc in range(4):
                fm_proj(cc * 128, 128, 1)
                P.op("act", act_fn(U[:, cc, :], psum[1][:, 0:128], AF.Gelu_apprx_tanh), reads=[PSB[1]], writes=[B_U])
            for cc in range(4):
                fm_proj(2048 + cc * 128, 128, 2)
                P.op("act", act_fn(TMPS[0], psum[2][:, 0:128], AF.Tanh, scale=0.5), reads=[PSB[2]], writes=[B_TMPS[0]])
                P.op("dve", stt(TMPS[1], TMPS[0], 1.0, psum[2][:, 0:128], ALU.add, ALU.mult), reads=[B_TMPS[0], PSB[2]], writes=[B_TMPS[1]])
                P.op("dve", ts(SG[:, cc, :], TMPS[1], HGN[:, cc:cc + 1], None, ALU.mult), reads=[B_TMPS[1], B_HGN], writes=[B_SG])
            P.op("act", act_fn(VG, psum[5][:, 0:512], AF.Gelu_apprx_tanh), reads=[PSB[5]], writes=[B_VG])
            for cc in range(2):
                fm_proj(1024 + cc * 128, 128, 1)
                P.op("act", act_fn(QF[:, cc, :], psum[1][:, 0:128], AF.Copy, scale=0.125), reads=[PSB[1]], writes=[B_QF])
                fm_proj(1280 + cc * 128, 128, 2)
                P.op("dve", cp(KF_[:, cc, :], psum[2][:, 0:128]), reads=[PSB[2]], writes=[B_KF])
            if t + 1 < NTILE:
                l0_block_norm(t + 1)
            P.op("dve", lambda e: e.bn_stats(out=LNST[:, 0:6], in_=VG), reads=[B_VG], writes=[B_LN])
            P.op("dve", lambda e: e.bn_aggr(out=LNMV[:, 0:2], in_=LNST[:, 0:6]), reads=[B_LN], writes=[B_LN])
            P.op("act", act_fn(LNMV[:, 2:3], LNMV[:, 1:2], AF.Ln, bias=EPS), reads=[B_LN], writes=[B_LN])
            P.op("act", act_fn(LNMV[:, 3:4], LNMV[:, 2:3], AF.Exp, scale=-0.5), reads=[B_LN], writes=[B_LN])
            P.op("dve", ts(VG, VG, LNMV[:, 0:1], LNMV[:, 3:4], ALU.subtract, ALU.mult), reads=[B_VG, B_LN], writes=[B_VG])
            P.op("dve", tt(VG, VG, LNG[:], ALU.mult), reads=[B_VG, B_MISC], writes=[B_VG])
            P.op("dve", tt(VTM, VG, LNB[:], ALU.add), reads=[B_VG, B_MISC], writes=[B_VTM])
            fns = []
            for g in range(4):
                fns.append(mm(psum[5][:, g * 128:(g + 1) * 128], VTM[:, g * 128:(g + 1) * 128], WST[:, g, :], True, False))
                fns.append(mm(psum[5][:, g * 128:(g + 1) * 128], ONESB[0:1, :], BSROW[:, g * 128:(g + 1) * 128], False, True))
            P.pe(fns, reads=[B_VTM, B_MISC, B_CST], writes=[PSB[5]])
            P.op("dve", tt(CAT[:, 0:4, :], psum[5][:].rearrange("p (g n) -> p g n", g=4), U[:], ALU.mult),
                 reads=[PSB[5], B_U], writes=[B_CAT])
            fns = []
            for r in range(2):
                tri = TRI_IF if r == 0 else TRI_IB
                for hp in range(2):
                    fns.append(mm(psum[6][:, (r * 2 + hp) * 128:(r * 2 + hp + 1) * 128],
                                  LA[:, r * 256 + hp * 128:r * 256 + (hp + 1) * 128], tri, True, True))
            P.pe(fns, reads=[B_LA, B_CST], writes=[PSB[6]])
            P.op("act", act_fn(EB, psum[6][:, 0:512], AF.Exp), reads=[PSB[6]], writes=[B_EB])
            P.op("act", act_fn(ENB, psum[6][:, 0:512], AF.Exp, scale=-1.0), reads=[PSB[6]], writes=[B_ENB])
            for r in range(2):
                for hh in range(2):
                    ps_ = slice(hh * 64, (hh + 1) * 64)
                    P.op("dve", tt(QEZ[ps_, r, hh::2, :], QF[ps_, :, :],
                                   EB[ps_, r * 256:(r + 1) * 256].rearrange("p (h n) -> p h n", h=2), ALU.mult),
                         reads=[B_QF, B_EB], writes=[B_QE])
                P.op("dve", tt(KE[:, r], KF_[:], ENB[:, r * 256:(r + 1) * 256].rearrange("p (h n) -> p h n", h=2), ALU.mult),
                     reads=[B_KF, B_ENB], writes=[B_KE])
            for r in range(2):
                fns = []
                for h in range(4):
                    hp = h // 2
                    fns.append(mm(psum[r][:, h * 128:(h + 1) * 128], KE[:, r, hp, :], QEZ[:, r, h, :], True, True))
                P.pe(fns, reads=[B_QE, B_KE], writes=[PSB[r]])
                P.op("dve", tt(ATT[:, r], psum[r][:].rearrange("p (h n) -> p h n", h=4),
                               MASKB16[r].unsqueeze(1).to_broadcast([128, 4, 128]), ALU.mult),
                     reads=[PSB[r], B_CST], writes=[B_ATT])
            P.op("act", act_fn(SIN16[:, 0], SF[:], AF.Copy), reads=[B_SF], writes=[B_SIN])
            P.op("act", act_fn(SIN16[:, 1], DSB[:, t], AF.Copy), reads=[DSB_B[t]], writes=[B_SIN])
            fns = []
            for h in range(4):
                hp, hh = h // 2, h % 2
                o_ap = psum[7][:, h * 128:(h + 1) * 128]
                fns.append(mm(o_ap, VVTM[:, h * 128:(h + 1) * 128], ATT[:, 0, h, :], True, False))
                fns.append(mm(o_ap, VVTM[:, h * 128:(h + 1) * 128], ATT[:, 1, h, :], False, False))
                fns.append(mm(o_ap, SIN16[:, 0, hp, :], QEZ[:, 0, h, :], False, False))
                fns.append(mm(o_ap, SIN16[:, 1, hp, :], QEZ[:, 1, h, :], False, True))
            P.pe(fns, reads=[B_VVTM, B_ATT, B_SIN, B_QE], writes=[PSB[7]])
            l0_kd_ds(0, 2, 3)
            l0_dec(0, DECF[:, 0:2], B_DECF, 2)
            gla_state_step(SF, B_SF, DECF, B_DECF, psum[3][:, 0:256], PSB[3])
            P.op("act", act_fn(OSQ, psum[7][:, 0:512], AF.Square), reads=[PSB[7]], writes=[B_OSQ])
            P.pe([mm(psum[4][:, 0:512], ONESB, OSQ, True, True)], reads=[B_OSQ, B_CST], writes=[PSB[4]])
            P.op("act", act_fn(OT, psum[4][:, 0:512], AF.Ln, bias=EPS, scale=1.0 / 128), reads=[PSB[4]], writes=[B_OT])
            P.op("act", act_fn(ORS, OT, AF.Exp, scale=-0.5), reads=[B_OT], writes=[B_ORS])
            P.op("dve", tt(OT, psum[7][:, 0:512], ORS, ALU.mult), reads=[PSB[7], B_ORS], writes=[B_OT])
            P.op("dve", tt(CAT[:, 4:8, :], OT.rearrange("p (h n) -> p h n", h=4), SG[:], ALU.mult),
                 reads=[B_OT, B_SG], writes=[B_CAT])
            out_proj_and_update(ev_w_out, t, 0, q)
        MODE["lnexp"] = False
        P.barrier()
        dump_x("l0mix")

        NSLOT = 6
        SLOT = [wslot(i * 4096, 8, 512) for i in range(NSLOT)]
        SLOT_B = [Buf("slot%d" % i) for i in range(NSLOT)]
        SLOT_S = [P.new_dsem() for _ in range(NSLOT)]
        SLOT_HS = [P.new_dsem() for _ in range(NSLOT)]

        def mlp_layer(l, final):
            SC.reset()
            MODE["lnexp"] = True
            HM5S = [SC.bf16(8 * 512).rearrange("p (c n) -> p c n", c=8) for _ in range(2)]
            H1f = SC.f32(16 * 256)
            H1 = H1f.bitcast(BF16).rearrange("p (c n) -> p c n", c=16)
            YS = SC.f32(8 * 512).rearrange("p (c n) -> p c n", c=8)
            RS5 = SC.f32(512)
            TM5 = [SC.f32(512), SC.f32(512)]
            ZB = [SC.f32(512), SC.f32(512)]
            OST = [H1f[:, 0:1024], H1f[:, 1024:2048]]
            assert SC.off <= NSC, SC.off
            B_HM5S, B_YS, B_RS5 = [Buf("hm5a"), Buf("hm5b")], Buf("ys"), Buf("rs5")
            B_H1 = [Buf("h1_%d" % i) for i in range(16)]
            B_TM5 = [Buf("tm5a"), Buf("tm5b")]
            B_ZB = [Buf("zba"), Buf("zbb")]
            OST_S = [P.new_dsem(), P.new_dsem()] if final else None
            pieces = []
            for sbk in range(NSB):
                for hf in range(2):
                    for i in range(4):
                        pieces.append(("w1", sbk, hf, i))
                    for i in range(4):
                        pieces.append(("w2", sbk, hf, i))

            B_WC = [Buf("wc%d" % i) for i in range(16)]
            WC_S = [P.new_dsem() for _ in range(4)]

            def load_piece(idx):
                kind, sbk, hf, i = pieces[idx]
                s = idx % NSLOT
                pid = hf * 8 + (0 if kind == "w1" else 4) + i
                flat = SLOT[s].rearrange("p k n -> p (k n)")
                if sbk > 0:
                    P.dma("sp", lambda e: e.dma_start(out=flat, in_=wcache.ap()[pid]), SLOT_HS[s], reads=[B_WC[pid]], writes=[SLOT_B[s]])
                    return
                if kind == "w1":
                    c0 = hf * 2048 + i * 512
                    load_w_rows(SLOT[s], mlp_w1.ap()[l], c0, 512, SLOT_S[s], SLOT_B[s])
                else:
                    dst = flat.rearrange("p (k n) -> p k n", k=16)
                    src = mlp_w2.ap()[l][hf * 2048:(hf + 1) * 2048, i * 256:(i + 1) * 256].rearrange("(k p) n -> p k n", p=128)
                    P.dma("pool", lambda e: e.dma_start(out=dst, in_=src), SLOT_S[s], writes=[SLOT_B[s]])
                P.dma("sp", lambda e: e.dma_start(out=wcache.ap()[pid], in_=flat), WC_S[pid % 4], reads=[SLOT_B[s]], writes=[B_WC[pid]])

            for idx in range(NSLOT - 1):
                load_piece(idx)
            nxt = NSLOT - 1
            idx = 0
            bank_rr = [0]
            def mlp_norm(sbk):
                qq = 0 if sbk == 0 else 1
                hb = sbk % 2
                norm_mod(xcols(sbk * 512, 512), xbufs(sbk * 512, 512), 512, l, 2, qq, HM5S[hb], B_HM5S[hb], HM5S[hb], B_HM5S[hb],
                         RS5, B_RS5, TM5, B_TM5, 7)

            mlp_norm(0)
            for sbk in range(NSB):
                q = 0 if sbk == 0 else 1
                c0 = sbk * 512
                HM5, B_HM5 = HM5S[sbk % 2], B_HM5S[sbk % 2]
                SQ5, B_SQ5 = HM5, B_HM5
                for hf in range(2):
                    for i in range(4):
                        s = idx % NSLOT
                        for cc in range(4):
                            hc = i * 4 + cc
                            bank = bank_rr[0] % 4
                            bank_rr[0] += 1
                            P.pe([mm(psum[bank][:, 0:512], SLOT[s][:, k, cc * 128:(cc + 1) * 128], HM5[:, k, :], k == 0, k == 7)
                                  for k in range(8)], reads=[SLOT_B[s], B_HM5], writes=[PSB[bank]])
                            z = hc % 2
                            P.op("act", act_fn(ZB[z], psum[bank][:, 0:512], AF.Identity,
                                               bias=pcol(1, 64 + l * 32 + hf * 16 + hc)),
                                 reads=[PSB[bank], B_PCOL], writes=[B_ZB[z]])
                            P.op("dve", stt(H1[:, hc, :], ZB[z], 0.0, ZB[z], ALU.max, ALU.mult), reads=[B_ZB[z]], writes=[B_H1[hc]])
                        idx += 1
                        if nxt < len(pieces):
                            load_piece(nxt)
                            nxt += 1
                    if hf == 1 and sbk + 1 < NSB:
                        mlp_norm(sbk + 1)
                    for i in range(4):
                        s = idx % NSLOT
                        w2v = SLOT[s].rearrange("p k n -> p (k n)").rearrange("p (k n) -> p k n", k=16)
                        for cc in range(2):
                            oc = i * 2 + cc
                            bank = 4 + (oc % 2)
                            P.pe([mm(psum[bank][:, 0:512], w2v[:, k, cc * 128:(cc + 1) * 128], H1[:, k, :], k == 0, k == 15)
                                  for k in range(16)], reads=[SLOT_B[s]] + B_H1, writes=[PSB[bank]])
                            if hf == 0:
                                P.op("act", act_fn(YS[:, oc, :], psum[bank][:, 0:512], AF.Identity, bias=pcol(0, 96 + l * 8 + oc)),
                                     reads=[PSB[bank], B_PCOL], writes=[B_YS])
                            else:
                                P.op("dve", tt(YS[:, oc, :], YS[:, oc, :], psum[bank][:, 0:512], ALU.add),
                                     reads=[PSB[bank], B_YS], writes=[B_YS])
                        idx += 1
                        if nxt < len(pieces):
                            load_piece(nxt)
                            nxt += 1
                P.op("act", act_fn(SQ5[:], YS[:], AF.Square), reads=[B_YS], writes=[B_SQ5])
                post_update([YS[:, c, :] for c in range(8)], [B_YS], c0, 512, l, 2, q, SQ5, B_SQ5, RS5, B_RS5, TM5, B_TM5, 7,
                            sq_done=True)
                if final:
                    for ti in range(4):
                        t = sbk * 4 + ti
                        o = t % 2
                        for half in range(2):
                            pb = 6 if half == 0 else 7
                            P.pe([lambda e, c=c, pb=pb, half=half, t=t: e.transpose(
                                psum[pb][:, (c - 4 * half) * 128:(c - 4 * half + 1) * 128], XT[:, c, t * 128:(t + 1) * 128], IDENT)
                                for c in range(4 * half, 4 * half + 4)], reads=[XBUF[t], B_CST], writes=[PSB[pb]])
                            ob = B_H1[o * 4 + half * 2:o * 4 + half * 2 + 2]
                            if half == 0:
                                P.op("act", act_fn(OST[o][:, 0:512], psum[pb][:, 0:512], AF.Copy), reads=[PSB[pb]], writes=ob)
                            else:
                                P.op("dve", cp(OST[o][:, 512:1024], psum[pb][:, 0:512]), reads=[PSB[pb]], writes=ob)
                        P.dma("sp", lambda e, t=t, o=o: e.dma_start(out=y_d.ap()[t * 128:(t + 1) * 128, :], in_=OST[o]),
                              OST_S[o], reads=B_H1[o * 4:o * 4 + 4])

        mlp_layer(0, False)
        MODE["lnexp"] = False
        P.barrier()
        dump_x("l0mlp")

        W_IN1 = wslot(0, 8, 2048)
        WIN1_B = [Buf("win1_%d" % i) for i in range(4)]
        WIN1_S = [P.new_dsem() for _ in range(4)]
        for i in range(4):
            load_w_rows(W_IN1[:, :, i * 512:(i + 1) * 512], rg_w_in.ap(), i * 512, 512, WIN1_S[i], WIN1_B[i])
        WOUT1_OFF = 8 * 2048
        W_OUTS1 = wslot(WOUT1_OFF, 8, 512)
        GOFF = WOUT1_OFF + 4096
        WGATE = WBt[:, GOFF:GOFF + 4096].rearrange("p (a r c n) -> p a r c n", a=2, r=2, c=8)
        B_WG = Buf("wgate")
        P.op("pool", lambda e: e.memset(WBt[:, GOFF:GOFF + 4096], 0.0), writes=[B_WG])
        for a, src in enumerate((rg_wa, rg_wx)):
            for r in range(2):
                for par in range(2):
                    dsg = P.new_dsem()
                    P.dma("pool", lambda e, a=a, r=r, par=par, src=src: e.dma_start(
                        out=WGATE[par * 64:(par + 1) * 64, a, r, :, par * 64:(par + 1) * 64],
                        in_=src.ap()[r].rearrange("(c two) i j -> two i c j", two=2)[par]), dsg, writes=[B_WG])

        SC.reset()
        HMS1 = [SC.bf16(8 * 256).rearrange("p (c n) -> p c n", c=8) for _ in range(2)]
        HM = HMS1[0]
        SQ = HM
        RSTD = SC.f32(256)
        TMPS = [SC.f32(256), SC.f32(256)]
        CAT = SC.bf16(8 * 256).rearrange("p (c n) -> p c n", c=8)
        XE = SC.f32(8 * 24).rearrange("p (c n) -> p c n", c=8)
        HME = SC.bf16(8 * 24).rearrange("p (c n) -> p c n", c=8)
        EDGE = SC.f32(8 * 24).rearrange("p (c n) -> p c n", c=8)
        HALO = SC.f32(8 * 3).rearrange("p (c n) -> p c n", c=8)
        XBH = [SC.f32(260), SC.f32(260)]
        XC = [SC.f32(256), SC.f32(256)]
        XCB = SC.bf16(8 * 256).rearrange("p (c n) -> p c n", c=8)
        GG = SC.bf16(8 * 256).rearrange("p (c n) -> p c n", c=8)
        THR = [[SC.f32(512).rearrange("p (c n) -> p c n", c=2) for _ in range(2)] for _ in range(2)]
        A2 = [[SC.f32(512).rearrange("p (c n) -> p c n", c=2) for _ in range(2)] for _ in range(2)]
        THI = [[SC.f32(512).rearrange("p (c n) -> p c n", c=2) for _ in range(2)] for _ in range(2)]
        HF = [SC.f32(512).rearrange("p (c n) -> p c n", c=2) for _ in range(2)]
        HBIAS = SC.f32(32).rearrange("p (a r c) -> p a r c", a=2, r=2)
        CLH = SC.f32(16).rearrange("p (r c) -> p r c", r=2)
        RACC = SC.f32(4)
        RUNF = SC.f32(8)
        RUNB = SC.f32(8)
        RUNP = SC.f32(8)
        RCV = SC.f32(16)
        PBLK = SC.f32(NMB * 8).rearrange("p (m c) -> p m c", c=8)
        FBLK = SC.f32(NMB * 8).rearrange("p (m c) -> p m c", c=8)
        SINB = SC.f32(NMB * 8).rearrange("p (m c) -> p m c", c=8)
        CPB = SC.f32(NMB * 8).rearrange("p (m c) -> p m c", c=8)
        NSR = SC.f32(32).rearrange("p (s r c) -> p s r c", s=2, r=2)
        NSRT = SC.f32(128)
        SEND = SC.f32(24).rearrange("p (c n) -> p c n", c=8)
        RC2 = SC.f32(48).rearrange("p (k c n) -> p k c n", k=2, c=8)
        SEND3 = SC.f32(16)
        RC3 = SC.f32(32).rearrange("p (k n) -> p k n", k=2)
        assert SC.off <= NSC, SC.off
        print("L1 scratch cols", SC.off, "of", NSC)
        (B_HM, B_RSTD, B_CAT, B_XE, B_HME, B_EDGE, B_HALO, B_XCB, B_GG, B_HF,
         B_RACC, B_RUNF, B_RUNB, B_RUNP, B_RCV, B_BLK, B_NSR, B_BROW, B_L1C) = [Buf("l1_%d" % i) for i in range(19)]
        B_TMPS = [Buf("tmp0"), Buf("tmp1")]
        B_XBH = [Buf("xbh0"), Buf("xbh1")]
        B_XC = [Buf("xc0"), Buf("xc1")]
        B_THR = [[Buf("thr"), Buf("thr")] for _ in range(2)]
        B_A2 = [[Buf("a2"), Buf("a2")] for _ in range(2)]
        B_THI = [[Buf("thi"), Buf("thi")] for _ in range(2)]
        B_HFS = [Buf("hf0"), Buf("hf1")]
        B_HMS1 = [B_HM, Buf("l1_hm1")]
        B_SQ = B_HM
        P.op("dve", ts(HBIAS[:].rearrange("p a r c -> p (a r c)"), PCOL[:, 2, 44:76], 0.5, None, ALU.mult),
             reads=[B_PCOL], writes=[B_L1C])
        P.op("dve", ts(CLH[:].rearrange("p r c -> p (r c)"), CLV[:, 0].rearrange("p r c -> p (r c)"), 0.5, None, ALU.mult),
             reads=[B_MISC], writes=[B_L1C])

        P.op("dve", cp(XE[:, :, 0:1], XT[:, :, 512:513]), reads=XBUF, writes=[B_XE])
        for b in range(2, 9):
            e0 = 1 + (b - 2) * 3
            t0 = (b + 1) * 256
            P.op("dve", cp(XE[:, :, e0:e0 + 3], XT[:, :, t0 - 2:t0 + 1]), reads=XBUF, writes=[B_XE])
        P.op("dve", cp(XE[:, :, 22:24], XT[:, :, NTOK - 2:NTOK]), reads=XBUF, writes=[B_XE])
        norm_mod(XE[:], [B_XE], 24, 1, 1, 1, HME, B_HME, HME, B_HME, RSTD, B_RSTD, TMPS, B_TMPS, 0)
        for c in range(8):
            P.pe([mm(psum[1][:, c * 24:(c + 1) * 24], W_IN1[:, k, c * 128:(c + 1) * 128], HME[:, k, :], k == 0, k == 7)
                  for k in range(8)], reads=[B_HME] + WIN1_B[0:2], writes=[PSB[1]])
        P.op("dve", cp(EDGE[:].rearrange("p c n -> p (c n)"), psum[1][:, 0:192]), reads=[PSB[1]], writes=[B_EDGE])
        B_SEND = Buf("send")
        P.op("dve", cp(SEND[:, :, 0:1], EDGE[:, :, 0:1]), reads=[B_EDGE], writes=[B_SEND])
        P.op("dve", cp(SEND[:, :, 1:3], EDGE[:, :, 22:24]), reads=[B_EDGE], writes=[B_SEND])
        B_CC2, B_CC2O = Buf("cc2"), Buf("cc2o")
        dq = P.new_dsem()
        P.dma("sp", lambda e: e.dma_start(out=cc2_in.ap(), in_=SEND.rearrange("p c n -> p (c n)")), dq, reads=[B_SEND], writes=[B_CC2])
        exchange(cc2_in, cc2_out, [B_CC2], B_CC2O)
        B_RC2 = Buf("rc2")
        dq = P.new_dsem()
        P.dma("sp", lambda e: e.dma_start(out=RC2.rearrange("p k c n -> p k (c n)"),
                                          in_=cc2_out.ap().rearrange("(k p) n -> p k n", p=128)), dq, reads=[B_CC2O], writes=[B_RC2])
        P.op("dve", ts(HALO[:, :, 0:2], RC2[:, 0, :, 1:3], FLG[:, 1:2], None, ALU.mult), reads=[B_RC2, B_MISC], writes=[B_HALO])
        P.op("dve", ts(HALO[:, :, 2:3], RC2[:, 1, :, 0:1], FLG[:, 0:1], None, ALU.mult), reads=[B_RC2, B_MISC], writes=[B_HALO])

        RINIT = lambda r: PCOL[:, 2, 92 + r * 8:92 + r * 8 + 8]

        def rg_norm(mb):
            hm, b_hm = HMS1[mb % 2], B_HMS1[mb % 2]
            norm_mod(xcols(mb * 256, 256), xbufs(mb * 256, 256), 256, 1, 1, cond_of_tile(2 * mb), hm, b_hm, hm, b_hm,
                     RSTD, B_RSTD, TMPS, B_TMPS, 0)

        def rg_block(mb, pass2):
            q = cond_of_tile(2 * mb)
            c0 = mb * 256
            HM, B_HM = HMS1[mb % 2], B_HMS1[mb % 2]
            for cpair in range(4):
                for z in range(2):
                    c = 2 * cpair + z
                    pb = 1 + z
                    P.pe([mm(psum[pb][:, 0:256], W_IN1[:, k, c * 128:(c + 1) * 128], HM[:, k, :], k == 0, k == 7) for k in range(8)],
                         reads=[B_HM, WIN1_B[c // 4]], writes=[PSB[pb]])
                    xbh, b_xbh = XBH[z], B_XBH[z]
                    P.op("act", act_fn(xbh[:, 2:258], psum[pb][:, 0:256], AF.Copy), reads=[PSB[pb]], writes=[b_xbh])
                    if mb < 2:
                        P.op("pool", lambda e, xbh=xbh: e.memset(xbh[:, 0:2], 0.0), writes=[b_xbh])
                        P.op("pool", lambda e, xbh=xbh: e.memset(xbh[:, 258:259], 0.0), writes=[b_xbh])
                    else:
                        if mb == 2:
                            P.op("pool", cp(xbh[:, 0:2], HALO[:, c, 0:2]), reads=[B_HALO], writes=[b_xbh])
                        else:
                            e0 = 1 + (mb - 3) * 3
                            P.op("pool", cp(xbh[:, 0:2], EDGE[:, c, e0:e0 + 2]), reads=[B_EDGE], writes=[b_xbh])
                        if mb == 9:
                            P.op("pool", cp(xbh[:, 258:259], HALO[:, c, 2:3]), reads=[B_HALO], writes=[b_xbh])
                        else:
                            e0 = 1 + (mb - 2) * 3 + 2
                            P.op("pool", cp(xbh[:, 258:259], EDGE[:, c, e0:e0 + 1]), reads=[B_EDGE], writes=[b_xbh])
                for j in range(4):
                    for z in range(2):
                        c = 2 * cpair + z
                        xbh, b_xbh, xc, b_xc = XBH[z], B_XBH[z], XC[z], B_XC[z]
                        if j == 0:
                            P.op("act", act_fn(xc, xbh[:, 0:256], AF.Identity, bias=pcol(2, 36 + c), scale=pcol(2, 4 + c)),
                                 reads=[b_xbh, B_PCOL], writes=[b_xc])
                        elif j < 3:
                            P.op("dve", stt(xc, xbh[:, j:j + 256], pcol(2, 4 + j * 8 + c), xc, ALU.mult, ALU.add),
                                 reads=[b_xbh, B_PCOL, b_xc], writes=[b_xc])
                        else:
                            P.op("dve", stt(XCB[:, c, :], xbh[:, 3:259], pcol(2, 4 + 24 + c), xc, ALU.mult, ALU.add),
                                 reads=[b_xbh, B_PCOL, b_xc], writes=[B_XCB])
            if pass2:
                for c2 in range(4):
                    for cc in range(2):
                        c = 2 * c2 + cc
                        P.pe([mm(psum[3][:, cc * 256:(cc + 1) * 256], W_IN1[:, k, 1024 + c * 128:1024 + (c + 1) * 128], HM[:, k, :],
                                 k == 0, k == 7) for k in range(8)], reads=[B_HM, WIN1_B[2 + c // 4]], writes=[PSB[3]])
                    P.op("act", act_fn(GG[:, 2 * c2:2 * c2 + 2, :], psum[3][:, 0:512].rearrange("p (c n) -> p c n", c=2),
                                       AF.Gelu_apprx_tanh), reads=[PSB[3]], writes=[B_GG])
            if mb + 1 < NMB:
                rg_norm(mb + 1)
            if mb < 2:
                P.op("dve", lambda e: e.memset(RUNF, 0.0), writes=[B_RUNF])
            if mb == 2:
                if pass2:
                    P.op("dve", cp(RUNF, RCV[:, 0:8]), reads=[B_RCV], writes=[B_RUNF])
                else:
                    P.op("dve", cp(RUNF, RINIT(0)), reads=[B_PCOL], writes=[B_RUNF])
            def grp_bufs(grp):
                sx = grp % 2
                return (2 * grp, THR[sx], A2[sx], THI[sx], HF[sx], B_THR[sx], B_A2[sx], B_THI[sx], B_HFS[sx], 4 + 2 * sx)

            def front(grp):
                ch0, thr, a2, thi, hf, b_thr, b_a2, b_thi, b_hf, pbank = grp_bufs(grp)
                for r in range(2):
                    for a_ in range(2):
                        fns = []
                        for ci in range(2):
                            c = ch0 + ci
                            fns.append(mm(psum[pbank + a_][:, ci * 256:(ci + 1) * 256], WGATE[:, a_, r, c, :], XCB[:, c, :], True, True))
                        P.pe(fns, reads=[B_XCB, B_WG], writes=[PSB[pbank + a_]])
                    for ci in range(2):
                        c = ch0 + ci
                        P.op("act", act_fn(thr[r][:, ci, :], psum[pbank][:, ci * 256:(ci + 1) * 256], AF.Tanh, scale=0.5,
                                           bias=HBIAS[:, 0, r, c:c + 1]), reads=[PSB[pbank], B_L1C], writes=[b_thr[r]])
                    for ci in range(2):
                        c = ch0 + ci
                        P.op("act", act_fn(thi[r][:, ci, :], psum[pbank + 1][:, ci * 256:(ci + 1) * 256], AF.Tanh, scale=0.5,
                                           bias=HBIAS[:, 1, r, c:c + 1]), reads=[PSB[pbank + 1], B_L1C], writes=[b_thi[r]])
                    P.op("dve", stt(thr[r][:], thr[r][:], 1.0, CLH[:, r, ch0:ch0 + 2].unsqueeze(2).to_broadcast([128, 2, 256]),
                                    ALU.add, ALU.mult), reads=[b_thr[r], B_L1C], writes=[b_thr[r]])
                    if (not pass2) and r == 1:
                        rs = slice(2 * (grp % 2), 2 * (grp % 2) + 2)
                        P.op("dve", lambda e, thr=thr, rs=rs: e.reduce_sum(out=RACC[:, rs], in_=thr[1][:], axis=mybir.AxisListType.X),
                             reads=[b_thr[1]], writes=[B_RACC])
                        P.op("act", act_fn(PBLK[:, mb, ch0:ch0 + 2], RACC[:, rs], AF.Exp), reads=[B_RACC], writes=[B_BLK])
                    P.op("act", act_fn(thr[r][:], thr[r][:], AF.Exp), reads=[b_thr[r]], writes=[b_thr[r]])
                    P.op("dve", tt(a2[r][:], thr[r][:], thr[r][:], ALU.mult), reads=[b_thr[r]], writes=[b_a2[r]])

            def back(grp):
                ch0, thr, a2, thi, hf, b_thr, b_a2, b_thi, b_hf, pbank = grp_bufs(grp)
                for r in range(2):
                    P.op("act", act_fn(a2[r][:], a2[r][:], AF.Sqrt, bias=1.0, scale=-1.0), reads=[b_a2[r]], writes=[b_a2[r]])
                for r in range(2):
                    P.op("dve", stt(thi[r][:], thi[r][:], 1.0, a2[r][:], ALU.add, ALU.mult), reads=[b_thi[r], b_a2[r]], writes=[b_thi[r]])
                for r in range(2):
                    P.op("dve", stt(thi[r][:], thi[r][:], 0.5, XCB[:, ch0:ch0 + 2, :], ALU.mult, ALU.mult),
                         reads=[b_thi[r], B_XCB], writes=[b_thi[r]])
                hbv = a2[0]
                for ci in range(2):
                    c = ch0 + ci
                    P.op("dve", lambda e, ci=ci, c=c: e.tensor_tensor_scan(
                        out=hf[:, ci, :], data0=thr[0][:, ci, :], data1=thi[0][:, ci, :], initial=RUNF[:, c:c + 1],
                        op0=ALU.mult, op1=ALU.add), reads=[b_thr[0], b_thi[0], B_RUNF], writes=[b_hf])
                    init = SINB[:, mb, c:c + 1] if pass2 else 0.0
                    P.op("dve", lambda e, ci=ci, init=init: e.tensor_tensor_scan(
                        out=hbv[:, ci, ::-1], data0=thr[1][:, ci, ::-1], data1=thi[1][:, ci, ::-1], initial=init,
                        op0=ALU.mult, op1=ALU.add), reads=[b_thr[1], b_thi[1], B_BLK], writes=[b_a2[0]])
                P.op("dve", cp(RUNF[:, ch0:ch0 + 2], hf[:, :, 255]), reads=[b_hf], writes=[B_RUNF])
                if (not pass2) or mb < 2:
                    P.op("dve", cp(FBLK[:, mb, ch0:ch0 + 2], hbv[:, :, 0]), reads=[b_a2[0]], writes=[B_BLK])
                if pass2:
                    P.op("dve", tt(hf[:], hf[:], hbv[:], ALU.add), reads=[b_hf, b_a2[0]], writes=[b_hf])
                    P.op("dve", tt(CAT[:, ch0:ch0 + 2, :], hf[:], GG[:, ch0:ch0 + 2, :], ALU.mult), reads=[b_hf, B_GG], writes=[B_CAT])

            front(0)
            for grp in range(4):
                if grp + 1 < 4:
                    front(grp + 1)
                back(grp)

        for mb in range(2):
            P.op("dve", lambda e, mb=mb: e.memset(SINB[:, mb, :], 0.0), writes=[B_BLK])
        rg_norm(2)
        for mb in range(2, NMB):
            rg_block(mb, False)
        P.op("dve", cp(RUNB, RINIT(1)), reads=[B_PCOL], writes=[B_RUNB])
        P.op("dve", lambda e: e.memset(RUNP, 1.0), writes=[B_RUNP])
        for mb in range(9, 1, -1):
            P.op("dve", cp(SINB[:, mb, :], RUNB), reads=[B_RUNB], writes=[B_BLK])
            P.op("dve", cp(CPB[:, mb, :], RUNP), reads=[B_RUNP], writes=[B_BLK])
            P.op("dve", tt(RUNB, RUNB, PBLK[:, mb, :], ALU.mult), reads=[B_RUNB, B_BLK], writes=[B_RUNB])
            P.op("dve", tt(RUNB, RUNB, FBLK[:, mb, :], ALU.add), reads=[B_RUNB, B_BLK], writes=[B_RUNB])
            P.op("dve", tt(RUNP, RUNP, PBLK[:, mb, :], ALU.mult), reads=[B_RUNP, B_BLK], writes=[B_RUNP])
        B_SEND3 = Buf("send3")
        P.op("dve", cp(SEND3[:, 0:8], RUNF), reads=[B_RUNF], writes=[B_SEND3])
        P.op("dve", cp(SEND3[:, 8:16], RUNB), reads=[B_RUNB], writes=[B_SEND3])
        B_CC3, B_CC3O = Buf("cc3"), Buf("cc3o")
        dq = P.new_dsem()
        P.dma("sp", lambda e: e.dma_start(out=cc3_in.ap(), in_=SEND3), dq, reads=[B_SEND3], writes=[B_CC3])
        exchange(cc3_in, cc3_out, [B_CC3], B_CC3O)
        B_RC3 = Buf("rc3")
        dq = P.new_dsem()
        P.dma("sp", lambda e: e.dma_start(out=RC3, in_=cc3_out.ap().rearrange("(k p) n -> p k n", p=128)), dq,
              reads=[B_CC3O], writes=[B_RC3])
        P.op("dve", stt(RCV[:, 0:8], RC3[:, 0, 0:8], FLG[:, 1:2], RINIT(0), ALU.mult, ALU.add), reads=[B_RC3, B_MISC, B_PCOL], writes=[B_RCV])
        P.op("dve", ts(RCV[:, 8:16], RC3[:, 1, 8:16], FLG[:, 0:1], None, ALU.mult), reads=[B_RC3, B_MISC], writes=[B_RCV])
        for mb in range(2, 10):
            P.op("dve", tt(CPB[:, mb, :], CPB[:, mb, :], RCV[:, 8:16], ALU.mult), reads=[B_BLK, B_RCV], writes=[B_BLK])
            P.op("dve", tt(SINB[:, mb, :], SINB[:, mb, :], CPB[:, mb, :], ALU.add), reads=[B_BLK], writes=[B_BLK])
        WS1 = WStream(WOUT1_OFF, rg_w_out)
        WS1.prefetch()

        def out_proj_and_update1(mb, q):
            WS1.project(CAT, B_CAT, 256, lambda oc: psum[4 + oc // 2][:, (oc % 2) * 256:(oc % 2) * 256 + 256],
                        lambda oc: PSB[4 + oc // 2], mb == NMB - 1)
            ych = [psum[4 + c // 2][:, (c % 2) * 256:(c % 2) * 256 + 256] for c in range(8)]
            post_update(ych, PSB[4:8], mb * 256, 256, 1, 1, q, SQ, B_SQ, RSTD, B_RSTD, TMPS, B_TMPS, 0,
                        y_all=PS[:, 2048:4096].rearrange("p (c n) -> p c n", c=8))

        rg_norm(0)
        for mb in range(NMB):
            SQ, B_SQ = HMS1[mb % 2], B_HMS1[mb % 2]
            rg_block(mb, True)
            if mb < 2:
                P.op("dve", cp(NSR[:, mb, 0, :], RUNF), reads=[B_RUNF], writes=[B_NSR])
                P.op("dve", cp(NSR[:, mb, 1, :], FBLK[:, mb, :]), reads=[B_BLK], writes=[B_NSR])
            out_proj_and_update1(mb, cond_of_tile(2 * mb))
            if mb == 1:
                P.pe([lambda e: e.transpose(psum[3][0:32, 0:128], NSR.rearrange("p s r c -> p (s r c)"), IDENT)],
                     reads=[B_NSR, B_CST], writes=[PSB[3]])
                B_NSRT = Buf("nsrt")
                P.op("dve", cp(NSRT[0:32, :], psum[3][0:32, 0:128]), reads=[PSB[3]], writes=[B_NSRT])
                dq = P.new_dsem()
                P.dma("sp", lambda e: e.dma_start(out=nsr_d.ap().rearrange("s r (c p) -> (s r c) p", p=128), in_=NSRT[0:32, :]), dq,
                      reads=[B_NSRT])


        P.barrier()
        dump_x("l1mix")

        mlp_layer(1, True)
        P.barrier()
        dump_x("final")
        P.barrier()

        with nc.Block() as block:
            @block.tensor
            def _(e):
                for f in P.st["pe"].ops:
                    f(e)

            @block.scalar
            def _(e):
                for f in P.st["act"].ops:
                    f(e)

            @block.vector
            def _(e):
                for f in P.st["dve"].ops:
                    f(e)

            @block.gpsimd
            def _(e):
                for f in P.st["pool"].ops:
                    f(e)

            @block.sync
            def _(e):
                for f in P.st["sp"].ops:
                    f(e)
    return nc, P.nops


def _prow(cidx, b, inp):
    g0 = np.zeros((128, 128), np.float32)
    g0[0:96] = inp["mod_b"].reshape(96, 128)
    g0[96:112] = inp["mlp_b2"].reshape(16, 128)
    g0[112:120] = inp["c_ctx"].reshape(8, 128)
    g0[120:128] = inp["c"][b].reshape(8, 128)
    g1 = np.zeros((128, 128), np.float32)
    g1[0:64] = inp["norm_g"].reshape(64, 128)
    g1[64:128] = inp["mlp_b1"].reshape(64, 128)
    g2 = np.zeros((128, 128), np.float32)
    g2[0:4] = inp["gla_norm_g"].reshape(4, 128)
    g2[4:36] = inp["rg_conv_w"].reshape(32, 128)
    g2[36:44] = inp["rg_conv_b"].reshape(8, 128)
    g2[44:60] = inp["rg_ba"].reshape(16, 128)
    g2[60:76] = inp["rg_bx"].reshape(16, 128)
    g2[76:92] = inp["rg_L"].reshape(16, 128)
    isA = (cidx % 2 == 0)
    st = inp["state_rglru"][b, 0]
    g2[92:100] = st[0].reshape(8, 128) if isA else 0.0
    g2[100:108] = 0.0 if isA else st[1].reshape(8, 128)
    return np.stack([g0, g1, g2])


_CACHE = {}


def kernel(**inputs):
    import os
    inp = {k: np.ascontiguousarray(np.asarray(v)) for k, v in inputs.items()}
    dbg = os.environ.get("KDBG")
    stop = os.environ.get("KSTOP")
    key = (dbg, stop)
    if key not in _CACHE:
        _CACHE[key] = build_program(dbg, stop)
    nc, nops = _CACHE[key]
    shared = {
        "mod_w": inp["mod_w"], "mlp_w1": inp["mlp_w1"], "mlp_w2": inp["mlp_w2"],
        "ev_w_in": inp["ev_w_in"][0], "ev_w_out": inp["ev_w_out"][0],
        "sgu_ln_g": inp["sgu_ln_g"].reshape(1, 512), "sgu_ln_b": inp["sgu_ln_b"].reshape(1, 512),
        "sgu_ws": inp["sgu_ws"][0], "sgu_bs": inp["sgu_bs"].reshape(1, 512),
        "gw2": inp["gla_gate_w2"][0], "gb": inp["gla_gate_b"].reshape(1, 512),
        "rgb": np.concatenate([inp["rg_ba"].reshape(-1), inp["rg_bx"].reshape(-1)]).reshape(1, 4096),
        "rg_w_in": inp["rg_w_in"][0], "rg_wa": inp["rg_wa"][0], "rg_wx": inp["rg_wx"][0], "rg_w_out": inp["rg_w_out"][0],
    }
    in_maps = []
    for cidx in range(NCORE):
        b = cidx // 2
        isA = (cidx % 2 == 0)
        half = cidx % 2
        xt = np.concatenate([inp["x_prompt"][2 * cidx].reshape(256, D), inp["x_prompt"][2 * cidx + 1].reshape(256, D),
                             inp["x_sample"][b, half * 2048:(half + 1) * 2048]], axis=0)
        flags = np.zeros((128, 4), np.float32)
        flags[:, 0] = 1.0 if isA else 0.0
        flags[:, 1] = 0.0 if isA else 1.0
        flags[:, 2] = 0.0 if isA else 32.0
        gi = np.zeros((2, 4, 64, 128), np.float32)
        if isA:
            gi[0] = inp["state_gla"][b, 0, 0]
        else:
            gi[1] = inp["state_gla"][b, 0, 1]
        m = dict(shared)
        m.update({"xtok": np.ascontiguousarray(xt), "prow": _prow(cidx, b, inp), "flags": flags, "ginit": gi})
        in_maps.append(m)
    res = run_bass_kernel_spmd(nc, in_maps, core_ids=list(range(NCORE)))
    R = res.results
    y_prompt = np.zeros((16, 256, D), np.float32)
    y_sample = np.zeros((4, 4096, D), np.float32)
    nsg = np.zeros((16, 1, 2, 4, 64, 128), np.float32)
    nsr = np.zeros((16, 1, 2, D), np.float32)
    for cidx in range(NCORE):
        r = R[cidx]
        b = cidx // 2
        half = cidx % 2
        y = r["y"]
        y_prompt[2 * cidx] = y[0:256]
        y_prompt[2 * cidx + 1] = y[256:512]
        y_sample[b, half * 2048:(half + 1) * 2048] = y[512:]
        nsg[2 * cidx:2 * cidx + 2, 0] = r["nsg"]
        nsr[2 * cidx:2 * cidx + 2, 0] = r["nsr"]
    if dbg:
        kernel.dbg = [R[c]["dbg"] for c in range(NCORE)]
    return (y_prompt, y_sample, nsg, nsr)
```

```python
import numpy as np
import concourse.bass as bass
import concourse.mybir as mybir
from concourse.bass_utils import run_bass_kernel_spmd

F32 = mybir.dt.float32
BF16 = mybir.dt.bfloat16
I32 = mybir.dt.int32
AF = mybir.ActivationFunctionType
ALU = mybir.AluOpType

NCORE = 8
D = 1024
NTOK = 2560
NTILE = 20
NMB = 10
NSB = 5
EPS = 1e-6
EV_IN = 2592
TWO_PI = 6.283185307179586
SELF_WAIT = True


class Buf:
    __slots__ = ("name", "w", "r")

    def __init__(self, name=""):
        self.name = name
        self.w = {}
        self.r = {}


class Stream:
    def __init__(self, name, sem):
        self.name = name
        self.sem = sem
        self.ops = []
        self.cnt = 0
        self.seen = {}


class DSem:
    def __init__(self, key, h):
        self.key = key
        self.h = h
        self.val = 0


class Prog:
    def __init__(self, sems, dsem_handles):
        self.st = {n: Stream(n, sems.get(n)) for n in ("pe", "act", "dve", "pool", "sp")}
        self.free_dsems = [DSem("d%d" % i, h) for i, h in enumerate(dsem_handles)]
        self.dsems = []
        self.nops = 0
        self.stopped = False
        self.nself = {}

    def new_dsem(self):
        d = self.free_dsems.pop()
        self.dsems.append(d)
        return d

    def _wait(self, st, tok):
        key, h, val = tok
        if st.seen.get(key, 0) >= val:
            return
        st.seen[key] = val
        st.ops.append(lambda e: e.wait_ge(h, val))

    def _deps(self, reads, writes):
        toks = []
        for b in reads:
            toks.extend(b.w.values())
        for b in writes:
            toks.extend(b.w.values())
            toks.extend(b.r.values())
        return toks

    def _mark(self, tok, reads, writes):
        for b in reads:
            b.r[tok[0]] = tok
        for b in writes:
            b.w = {tok[0]: tok}
            b.r = {}

    def op(self, eng, fn, reads=(), writes=()):
        if self.stopped:
            return None
        st = self.st[eng]
        for t in self._deps(reads, writes):
            if t[0] == eng and not SELF_WAIT:
                continue
            if t[0] == eng and st.seen.get(eng, 0) < t[2]:
                self.nself[eng] = self.nself.get(eng, 0) + 1
            self._wait(st, t)
        st.cnt += 1
        sem = st.sem
        tok = (eng, sem, st.cnt)
        st.ops.append(lambda e: fn(e).then_inc(sem, 1))
        self._mark(tok, reads, writes)
        self.nops += 1
        return tok

    def pe(self, fns, reads=(), writes=()):
        if self.stopped:
            return None
        st = self.st["pe"]
        for t in self._deps(reads, writes):
            if t[0] == "pe":
                continue
            self._wait(st, t)
        st.cnt += 1
        sem = st.sem
        tok = ("pe", sem, st.cnt)
        for f in fns[:-1]:
            st.ops.append(f)
        last = fns[-1]
        st.ops.append(lambda e: last(e).then_inc(sem, 1))
        self._mark(tok, reads, writes)
        self.nops += len(fns)
        return tok

    def dma(self, q, fn, dsem, reads=(), writes=()):
        if self.stopped:
            return None
        st = self.st[q]
        toks = self._deps(reads, writes)
        if dsem.val > 0:
            toks.append((dsem.key, dsem.h, dsem.val))
        for t in toks:
            self._wait(st, t)
        dsem.val += 16
        h = dsem.h
        tok = (dsem.key, h, dsem.val)
        st.ops.append(lambda e: fn(e).then_inc(h, 16))
        self._mark(tok, reads, writes)
        self.nops += 1
        return tok

    def barrier(self):
        if self.stopped:
            return
        toks = [(n, s.sem, s.cnt) for n, s in self.st.items() if s.cnt > 0]
        toks += [(d.key, d.h, d.val) for d in self.dsems if d.val > 0]
        for st in self.st.values():
            for t in toks:
                if t[0] != st.name:
                    self._wait(st, t)


class Arena:
    def __init__(self, t, n):
        self.t = t
        self.n = n
        self.off = 0

    def reset(self):
        self.off = 0

    def f32(self, n):
        assert self.off + n <= self.n, ("arena overflow", self.off, n, self.n)
        ap = self.t[:, self.off:self.off + n]
        self.off += n
        return ap

    def bf16(self, n):
        n32 = (n + 1) // 2
        return self.f32(n32).bitcast(BF16)[:, 0:n]


def act_fn(out, in_, func, bias=None, scale=None, accum_out=None):
    kw = {}
    if bias is not None:
        kw["bias"] = bias
    if scale is not None:
        kw["scale"] = scale
    if accum_out is not None:
        kw["accum_out"] = accum_out
    return lambda e: e.activation(out=out, in_=in_, func=func, **kw)


def mm(out, lhsT, rhs, start, stop):
    return lambda e: e.matmul(out, lhsT, rhs, start=start, stop=stop)


def tt(out, in0, in1, op):
    return lambda e: e.tensor_tensor(out=out, in0=in0, in1=in1, op=op)


def ts(out, in0, s1, s2, op0, op1=None):
    if op1 is None:
        return lambda e: e.tensor_scalar(out=out, in0=in0, scalar1=s1, scalar2=None, op0=op0)
    return lambda e: e.tensor_scalar(out=out, in0=in0, scalar1=s1, scalar2=s2, op0=op0, op1=op1)


def stt(out, in0, scalar, in1, op0, op1):
    return lambda e: e.scalar_tensor_tensor(out=out, in0=in0, scalar=scalar, in1=in1, op0=op0, op1=op1)


def cp(out, in_):
    return lambda e: e.tensor_copy(out=out, in_=in_)


class _Stop(Exception):
    pass


def build_program(debug_phase=None, stop_after=None):
    nc = bass.Bass("TRN2", target_bir_lowering=False)

    def din(name, shape, dt=F32):
        return nc.dram_tensor(name, list(shape), dt, kind="ExternalInput")

    def dout(name, shape, dt=F32):
        return nc.dram_tensor(name, list(shape), dt, kind="ExternalOutput")

    xtok = din("xtok", [NTOK, D])
    prow = din("prow", [3, 128, 128])
    flags_d = din("flags", [128, 4])
    ginit_d = din("ginit", [2, 4, 64, 128])
    mod_w = din("mod_w", [2, D, 6 * D])
    mlp_w1 = din("mlp_w1", [2, D, 4 * D])
    mlp_w2 = din("mlp_w2", [2, 4 * D, D])
    ev_w_in = din("ev_w_in", [D, EV_IN])
    ev_w_out = din("ev_w_out", [D, D])
    lng_d = din("sgu_ln_g", [1, 512])
    lnb_d = din("sgu_ln_b", [1, 512])
    ws_d = din("sgu_ws", [4, 128, 128])
    bs_d = din("sgu_bs", [1, 512])
    gw2_d = din("gw2", [2, 16, 256])
    gb_d = din("gb", [1, 512])
    rg_w_in = din("rg_w_in", [D, 2 * D])
    rg_wa = din("rg_wa", [2, 16, 64, 64])
    rg_wx = din("rg_wx", [2, 16, 64, 64])
    rg_w_out = din("rg_w_out", [D, D])
    rgb_d = din("rgb", [1, 4096])

    y_d = dout("y", [NTOK, D])
    nsg_d = dout("nsg", [2, 2, 4, 64, 128])
    nsr_d = dout("nsr", [2, 2, D])
    dbg_d = dout("dbg", [128, 8, NTOK]) if debug_phase is not None else None

    cc1_in = nc.dram_tensor("cc1_in", [256, 256], F32)
    cc1_out = nc.dram_tensor("cc1_out", [512, 256], F32)
    cc2_in = nc.dram_tensor("cc2_in", [128, 24], F32)
    cc2_out = nc.dram_tensor("cc2_out", [256, 24], F32)
    cc3_in = nc.dram_tensor("cc3_in", [128, 16], F32)
    cc3_out = nc.dram_tensor("cc3_out", [256, 16], F32)
    PAIRS = [[0, 1], [2, 3], [4, 5], [6, 7]]
    wcache = nc.dram_tensor("wcache", [16, 128, 4096], BF16)
    wocache = nc.dram_tensor("wocache", [4, 128, 2048], BF16)

    import contextlib
    es = contextlib.ExitStack()
    with es:
        def sb(name, shape, dt=F32):
            return es.enter_context(nc.sbuf_tensor(name, list(shape), dt))

        XT = sb("XT", [128, 8, NTOK])
        NWB = 24832
        WBt = sb("WB", [128, NWB], BF16)
        NSC = 15296
        SCt = sb("SC", [128, NSC])
        CST = sb("CST", [128, 5, 128])
        NEGC = sb("NEGC", [128, 2])
        CBF = sb("CBF", [128, 3, 128], BF16)
        PCOL = sb("PCOL", [128, 3, 128])
        MODV = sb("MODV", [128, 2, 6, 8, 2])
        DER = sb("DER", [128, 2, 6, 2, 8])
        LNG = sb("LNG", [128, 512])
        LNB = sb("LNB", [128, 512])
        GWBD = sb("GWBD", [32, 512], BF16)
        GBROW = sb("GBROW", [1, 512], BF16)
        BSROW = sb("BSROW", [1, 512], BF16)
        WST = sb("WST", [128, 4, 128], BF16)
        GINIT = sb("GINIT", [128, 2, 2, 128])
        FLG = sb("FLG", [128, 4])
        SCOND = sb("SCOND", [128, 8, 2])
        SCONDB = sb("SCONDB", [128, 8, 2], BF16)
        CLV = sb("CLV", [128, 2, 2, 8])
        PS = es.enter_context(nc.psum_tensor("ps", [128, 4096], F32))
        psum = [PS[:, i * 512:(i + 1) * 512] for i in range(8)]
        PSB = [Buf("ps%d" % i) for i in range(8)]

        eng_sems = {n: es.enter_context(nc.semaphore("s_" + n)) for n in ("pe", "act", "dve", "pool")}
        dsem_h = [es.enter_context(nc.semaphore("dq%d" % i)) for i in range(90)]
        cc_sem = es.enter_context(nc.semaphore("cc"))
        P = Prog(eng_sems, dsem_h)
        SC = Arena(SCt, NSC)

        IDENT = CST[:, 0, :]
        TRI_IF = CST[:, 1, :]
        TRI_IB = CST[:, 2, :]
        TRI_SF = CST[:, 3, :]
        TRI_SB = CST[:, 4, :]
        ONESB = CBF[:, 0, :]
        MASKB16 = [CBF[:, 1, :], CBF[:, 2, :]]
        B_CST = Buf("cst")
        B_PCOL = Buf("pcol")
        B_MISC = Buf("misc")
        B_MOD = Buf("mod")
        B_DER = Buf("der")
        XBUF = [Buf("x%d" % i) for i in range(NTILE)]

        def pcol(g, row, n=1):
            return PCOL[:, g, row:row + n]

        ds_misc = P.new_dsem()
        SC.reset()
        IOT_I = SC.f32(128).bitcast(I32)
        IOT_F = SC.f32(128)
        KI = SC.f32(2).bitcast(I32)
        KF = SC.f32(2)
        OMEGA = SC.f32(2)
        RI = SC.f32(64).bitcast(I32)
        RF = SC.f32(64)
        ANG = SC.f32(4 * 64).rearrange("p (a b) -> p a b", a=4)
        ANGK = SC.f32(4 * 64).rearrange("p (a b) -> p a b", a=4)
        ANGKI = SC.f32(4 * 64).bitcast(I32).rearrange("p (a b) -> p a b", a=4)
        POS = SC.f32(64)
        PEROW = SC.f32(4 * 32).rearrange("p (a b) -> p a b", a=4)
        PECOL = SC.f32(4 * 64).rearrange("p (a b) -> p a b", a=4)
        PROW = SC.f32(3 * 128).rearrange("p (a b) -> p a b", a=3)
        WSS = SC.f32(4 * 128).rearrange("p (a b) -> p a b", a=4)
        B_T = Buf("setup_tmp")
        B_PROW = Buf("prow")
        B_WSS = Buf("wss")

        P.dma("sp", lambda e: e.dma_start(out=PROW, in_=prow.ap().rearrange("g r c -> r g c")), ds_misc, writes=[B_PROW])
        d2 = P.new_dsem()
        P.dma("sp", lambda e: e.dma_start(out=FLG[:], in_=flags_d.ap()), d2, writes=[B_MISC])
        d3 = P.new_dsem()
        P.dma("sp", lambda e: e.dma_start(out=LNG[:], in_=lng_d.ap().to_broadcast([128, 512])), d3, writes=[B_MISC])
        d4 = P.new_dsem()
        P.dma("sp", lambda e: e.dma_start(out=LNB[:], in_=lnb_d.ap().to_broadcast([128, 512])), d4, writes=[B_MISC])
        d5 = P.new_dsem()
        P.dma("sp", lambda e: e.dma_start(out=WSS, in_=ws_d.ap().rearrange("g p q -> p g q")), d5, writes=[B_WSS])
        d6 = P.new_dsem()
        P.dma("sp", lambda e: e.dma_start(
            out=GINIT[:], in_=ginit_d.ap().rearrange("r (hp hh) d v -> (hh d) r hp v", hh=2)), d6, writes=[B_MISC])
        P.op("pool", lambda e: e.memset(GWBD[:], 0.0), writes=[B_MISC])
        d7 = P.new_dsem()
        P.dma("pool", lambda e: e.dma_start(out=GWBD[0:16, 0:256], in_=gw2_d.ap()[0]), d7, writes=[B_MISC])
        d8 = P.new_dsem()
        P.dma("pool", lambda e: e.dma_start(out=GWBD[16:32, 256:512], in_=gw2_d.ap()[1]), d8, writes=[B_MISC])
        d9 = P.new_dsem()
        P.dma("pool", lambda e: e.dma_start(out=GBROW[:], in_=gb_d.ap()), d9, writes=[B_MISC])
        d10 = P.new_dsem()
        P.dma("pool", lambda e: e.dma_start(out=BSROW[:], in_=bs_d.ap()), d10, writes=[B_MISC])

        P.op("pool", lambda e: e.iota(IOT_I, pattern=[[1, 128]], base=0, channel_multiplier=-1), writes=[B_T])
        P.op("dve", cp(IOT_F, IOT_I), reads=[B_T], writes=[B_T])
        P.op("dve", ts(IDENT, IOT_F, 0.0, None, ALU.is_equal), reads=[B_T], writes=[B_CST])
        P.op("dve", ts(TRI_IF, IOT_F, 0.0, -1.0 / 16, ALU.is_ge, ALU.mult), reads=[B_T], writes=[B_CST])
        P.op("dve", ts(TRI_IB, IOT_F, 0.0, -1.0 / 16, ALU.is_le, ALU.mult), reads=[B_T], writes=[B_CST])
        P.op("dve", ts(TRI_SF, IOT_F, 0.0, -1.0 / 16, ALU.is_lt, ALU.mult), reads=[B_T], writes=[B_CST])
        P.op("dve", ts(TRI_SB, IOT_F, 0.0, -1.0 / 16, ALU.is_gt, ALU.mult), reads=[B_T], writes=[B_CST])
        P.op("dve", ts(MASKB16[0], IOT_F, 0.0, None, ALU.is_ge), reads=[B_T], writes=[B_CST])
        P.op("dve", ts(MASKB16[1], IOT_F, 0.0, None, ALU.is_le), reads=[B_T], writes=[B_CST])
        P.op("dve", lambda e: e.memset(ONESB, 1.0), writes=[B_CST])
        P.op("dve", lambda e: e.memset(NEGC[:], -1.0 / 16), writes=[B_CST])

        for g in range(3):
            P.pe([lambda e, g=g: e.transpose(psum[0][:, g * 128:(g + 1) * 128], PROW[:, g, :], IDENT)],
                 reads=[B_PROW, B_CST], writes=[PSB[0]])
        P.op("dve", cp(PCOL[:].rearrange("p a b -> p (a b)"), psum[0][:, 0:384]), reads=[PSB[0]], writes=[B_PCOL])
        for g in range(4):
            P.pe([lambda e, g=g: e.transpose(psum[1][:, g * 128:(g + 1) * 128], WSS[:, g, :], IDENT)],
                 reads=[B_WSS, B_CST], writes=[PSB[1]])
        P.op("dve", cp(WST[:].rearrange("p a b -> p (a b)"), psum[1][:, 0:512]), reads=[PSB[1]], writes=[B_MISC])
        P.op("act", act_fn(SCOND[:].rearrange("p c q -> p q c"),
                           PCOL[:, 0, 112:128].rearrange("p (q c) -> p q c", q=2), AF.Silu),
             reads=[B_PCOL], writes=[B_MISC])
        P.op("dve", cp(SCONDB[:], SCOND[:]), reads=[B_MISC], writes=[B_MISC])
        CLT = SC.f32(16)
        P.op("act", act_fn(CLT, PCOL[:, 2, 76:92], AF.Exp, scale=-1.0), reads=[B_PCOL], writes=[B_T])
        P.op("act", act_fn(CLT, CLT, AF.Ln, bias=1.0), reads=[B_T], writes=[B_T])
        P.op("dve", ts(CLV[:, 0].rearrange("p a b -> p (a b)"), CLT, -8.0, None, ALU.mult), reads=[B_T], writes=[B_MISC])
        P.op("dve", ts(CLV[:, 1].rearrange("p a b -> p (a b)"), CLT, -16.0, None, ALU.mult), reads=[B_T], writes=[B_MISC])

        P.op("pool", lambda e: e.iota(KI, pattern=[[128, 2]], base=0, channel_multiplier=1), writes=[B_T])
        P.op("dve", cp(KF, KI), reads=[B_T], writes=[B_T])
        P.op("act", act_fn(OMEGA, KF, AF.Exp, scale=-float(np.log(10000.0)) / 256.0), reads=[B_T], writes=[B_T])
        P.op("pool", lambda e: e.iota(RI, pattern=[[1, 64]], base=0, channel_multiplier=0), writes=[B_T])
        P.op("dve", cp(RF, RI), reads=[B_T], writes=[B_T])

        def sincos_table(dst, n, rowoff):
            if rowoff:
                P.op("dve", ts(POS[:, 0:n], RF[:, 0:n], FLG[:, 2:3], None, ALU.add), reads=[B_T, B_MISC], writes=[B_T])
            else:
                P.op("dve", cp(POS[:, 0:n], RF[:, 0:n]), reads=[B_T], writes=[B_T])
            for c2 in range(2):
                P.op("dve", ts(ANG[:, c2, 0:n], POS[:, 0:n], OMEGA[:, c2:c2 + 1], None, ALU.mult), reads=[B_T], writes=[B_T])
                P.op("dve", ts(ANG[:, 2 + c2, 0:n], POS[:, 0:n], OMEGA[:, c2:c2 + 1], 0.5 * np.pi, ALU.mult, ALU.add),
                     reads=[B_T], writes=[B_T])
            P.op("dve", ts(ANGKI[:, :, 0:n], ANG[:, :, 0:n], 1.0 / TWO_PI, None, ALU.mult), reads=[B_T], writes=[B_T])
            P.op("dve", cp(ANGK[:, :, 0:n], ANGKI[:, :, 0:n]), reads=[B_T], writes=[B_T])
            P.op("dve", stt(ANG[:, :, 0:n], ANGK[:, :, 0:n], -TWO_PI, ANG[:, :, 0:n], ALU.mult, ALU.add),
                 reads=[B_T], writes=[B_T])
            P.op("dve", ts(ANG[:, :, 0:n], ANG[:, :, 0:n], 3.14159, -3.14159, ALU.min, ALU.max), reads=[B_T], writes=[B_T])
            P.op("act", act_fn(dst, ANG[:, :, 0:n], AF.Sin), reads=[B_T], writes=[B_MISC])

        sincos_table(PECOL[:], 64, False)
        sincos_table(PEROW[:], 32, True)

        def wslot(off, kc, ncol):
            return WBt[:, off:off + kc * ncol].rearrange("p (k n) -> p k n", k=kc)

        W_IN0 = wslot(0, 8, EV_IN)
        WIN0_B = [Buf("win0_%d" % i) for i in range(6)]
        WIN0_S = [P.new_dsem() for _ in range(6)]
        WOUT_OFF = 8 * EV_IN
        W_OUTS = wslot(WOUT_OFF, 8, 512)
        WOUT_B = Buf("wouts")
        WOUT_S = P.new_dsem()

        def load_w_rows(dst, src_ap, c0, ncol, dsem, buf, q="pool"):
            P.dma(q, lambda e: e.dma_start(out=dst, in_=src_ap.rearrange("(k p) n -> p k n", p=128)[:, :, c0:c0 + ncol]),
                  dsem, writes=[buf])

        class WStream:
            def __init__(self, off, src):
                self.slots = [WBt[:, off + i * 2048:off + (i + 1) * 2048].rearrange("p (k n) -> p k n", k=8) for i in range(2)]
                self.bufs = [Buf("wq0"), Buf("wq1")]
                self.sems = [P.new_dsem(), P.new_dsem()]
                self.hsems = [P.new_dsem(), P.new_dsem()]
                self.csems = [P.new_dsem(), P.new_dsem()]
                self.cbufs = [Buf("woc%d" % i) for i in range(4)]
                self.src = src
                self.issued = 0
                self.base = 0

            def load(self):
                qi = self.issued
                self.issued += 1
                qt, sl = qi % 4, qi % 2
                flat = self.slots[sl].rearrange("p k n -> p (k n)")
                if qi < 4:
                    load_w_rows(self.slots[sl], self.src.ap(), qt * 256, 256, self.sems[sl], self.bufs[sl])
                    P.dma("sp", lambda e: e.dma_start(out=wocache.ap()[qt], in_=flat), self.csems[sl],
                          reads=[self.bufs[sl]], writes=[self.cbufs[qt]])
                else:
                    P.dma("sp", lambda e: e.dma_start(out=flat, in_=wocache.ap()[qt]), self.hsems[sl],
                          reads=[self.cbufs[qt]], writes=[self.bufs[sl]])

            def prefetch(self):
                while self.issued < self.base + 2:
                    self.load()

            def project(self, cat, b_cat, n, out_ap, out_buf, last):
                for q_ in range(4):
                    while self.issued < min(self.base + q_ + 2, self.base + 4):
                        self.load()
                    sl = (self.base + q_) % 2
                    for o2 in range(2):
                        oc = q_ * 2 + o2
                        P.pe([mm(out_ap(oc), self.slots[sl][:, k, o2 * 128:(o2 + 1) * 128], cat[:, k, 0:n], k == 0, k == 7)
                              for k in range(8)], reads=[b_cat, self.bufs[sl]], writes=[out_buf(oc)])
                self.base += 4
                if not last:
                    self.prefetch()

        def win0_bufs(lo, hi):
            return [WIN0_B[i] for i in range(lo // 512, (hi - 1) // 512 + 1)]

        def load_win0():
            for i in (2, 3, 4, 5, 0, 1):
                c0 = i * 512
                ncol = min(512, EV_IN - c0)
                load_w_rows(W_IN0[:, :, c0:c0 + ncol], ev_w_in.ap(), c0, ncol, WIN0_S[i], WIN0_B[i])

        NMW = 4
        MW = [SC.bf16(8 * 512).rearrange("p (k n) -> p k n", k=8) for _ in range(NMW)]
        MW_B = [Buf("mw%d" % i) for i in range(NMW)]
        MW_S = [P.new_dsem() for _ in range(NMW)]
        STG = [SC.f32(1024) for _ in range(2)]
        STG_B = [Buf("stg%d" % i) for i in range(2)]
        STG_S = [P.new_dsem() for _ in range(2)]
        assert SC.off <= NSC

        def mod_piece(l, pc, mw, mw_b, mw_s, pb):
            P.dma("pool", lambda e: e.dma_start(
                out=mw, in_=mod_w.ap()[l].rearrange("(k p) n -> p k n", p=128)[:, :, pc * 512:(pc + 1) * 512]),
                mw_s, writes=[mw_b])
            fns = []
            for cc in range(4):
                for k in range(8):
                    fns.append(mm(psum[pb][:, 256 + cc * 2:256 + cc * 2 + 2], mw[:, k, cc * 128:(cc + 1) * 128], SCONDB[:, k, :],
                                  k == 0, k == 7))
            P.pe(fns, reads=[mw_b, B_MISC], writes=[PSB[pb]])
            for cc in range(4):
                col = pc * 4 + cc
                j, c = col // 8, col % 8
                P.op("dve", ts(MODV[:, l, j, c, :], psum[pb][:, 256 + cc * 2:256 + cc * 2 + 2], pcol(0, l * 48 + col), None, ALU.add),
                     reads=[PSB[pb], B_PCOL], writes=[B_MOD])

        def derive(l):
            for q in range(2):
                mv = lambda j: MODV[:, l, j, :, q]
                ng = lambda j: PCOL[:, 1, l * 32 + j * 8:l * 32 + j * 8 + 8]
                P.op("dve", stt(DER[:, l, 0, q, :], mv(1), 1.0, ng(0), ALU.add, ALU.mult), reads=[B_MOD, B_PCOL], writes=[B_DER])
                P.op("dve", cp(DER[:, l, 1, q, :], mv(0)), reads=[B_MOD], writes=[B_DER])
                P.op("dve", tt(DER[:, l, 2, q, :], mv(2), ng(1), ALU.mult), reads=[B_MOD, B_PCOL], writes=[B_DER])
                P.op("dve", stt(DER[:, l, 3, q, :], mv(4), 1.0, ng(2), ALU.add, ALU.mult), reads=[B_MOD, B_PCOL], writes=[B_DER])
                P.op("dve", cp(DER[:, l, 4, q, :], mv(3)), reads=[B_MOD], writes=[B_DER])
                P.op("dve", tt(DER[:, l, 5, q, :], mv(5), ng(3), ALU.mult), reads=[B_MOD, B_PCOL], writes=[B_DER])

        def load_tile(t):
            s = t % 2
            P.dma("sp", lambda e: e.dma_start(out=STG[s], in_=xtok.ap()[t * 128:(t + 1) * 128, :]), STG_S[s], writes=[STG_B[s]])
            for half in range(2):
                pb = 3 + half
                fns = [lambda e, c=c, pb=pb, half=half: e.transpose(
                    psum[pb][:, (c - 4 * half) * 128:(c - 4 * half + 1) * 128], STG[s][:, c * 128:(c + 1) * 128], IDENT)
                    for c in range(4 * half, 4 * half + 4)]
                P.pe(fns, reads=[STG_B[s], B_CST], writes=[PSB[pb]])
            xs = XT[:, :, t * 128:(t + 1) * 128]
            if t < 4:
                for half in range(2):
                    P.op("dve" if half == 0 else "act",
                         (cp(xs[:, 4 * half:4 * half + 4, :], psum[3 + half][:].rearrange("p (c n) -> p c n", c=4))
                          if half == 0 else
                          act_fn(xs[:, 4 * half:4 * half + 4, :], psum[3 + half][:].rearrange("p (c n) -> p c n", c=4), AF.Copy)),
                         reads=[PSB[3 + half]], writes=[XBUF[t]])
            else:
                r0 = (t - 4) * 2
                for c in range(4):
                    for rr in range(2):
                        P.op("dve", ts(xs[:, c, rr * 64:(rr + 1) * 64], psum[3][:, c * 128 + rr * 64:c * 128 + rr * 64 + 64],
                                       PEROW[:, c, r0 + rr:r0 + rr + 1], None, ALU.add),
                             reads=[PSB[3], B_MISC], writes=[XBUF[t]])
                for c in range(4):
                    P.op("dve", tt(xs[:, 4 + c, :].rearrange("p (r n) -> p r n", r=2),
                                   psum[4][:, c * 128:(c + 1) * 128].rearrange("p (r n) -> p r n", r=2),
                                   PECOL[:, c, :].unsqueeze(1).to_broadcast([128, 2, 64]), ALU.add),
                         reads=[PSB[4], B_MISC], writes=[XBUF[t]])

        for pc in range(12):
            mod_piece(0, pc, MW[pc % NMW], MW_B[pc % NMW], MW_S[pc % NMW], 2)
        derive(0)
        load_win0()
        for t in range(NTILE):
            load_tile(t)

        def xcols(c0, n):
            return XT[:, :, c0:c0 + n]

        def xbufs(c0, n):
            return XBUF[c0 // 128:(c0 + n - 1) // 128 + 1]

        MODE = {"lnexp": False}

        def rstd_from(ps_ap, n, scale, tmp, tmp_b, rstd, rstd_b, ps_b):
            if MODE["lnexp"]:
                P.op("act", act_fn(tmp[:, 0:n], ps_ap, AF.Ln, bias=EPS, scale=scale), reads=[ps_b], writes=[tmp_b])
                P.op("act", act_fn(rstd[:, 0:n], tmp[:, 0:n], AF.Exp, scale=-0.5), reads=[tmp_b], writes=[rstd_b])
            else:
                P.op("act", act_fn(tmp[:, 0:n], ps_ap, AF.Sqrt, bias=EPS, scale=scale), reads=[ps_b], writes=[tmp_b])
                P.op("dve", lambda e: e.reciprocal(out=rstd[:, 0:n], in_=tmp[:, 0:n]), reads=[tmp_b], writes=[rstd_b])

        def norm_mod(src, src_bufs, n, l, which, q, HM, HM_B, SQ, SQ_B, RSTD, RSTD_B, TMPS, TMP_BS, pb):
            ja, jb = (0, 1) if which == 1 else (3, 4)
            P.op("act", act_fn(SQ[:, :, 0:n], src, AF.Square), reads=src_bufs, writes=[SQ_B])
            P.pe([mm(psum[pb][:, 0:n], ONESB, SQ[:, c, 0:n], c == 0, c == 7) for c in range(8)],
                 reads=[SQ_B, B_CST], writes=[PSB[pb]])
            rstd_from(psum[pb][:, 0:n], n, 1.0 / D, TMPS[0], TMP_BS[0], RSTD, RSTD_B, PSB[pb])
            for c in range(8):
                k = c % 2
                P.op("dve", tt(TMPS[k][:, 0:n], src[:, c, :], RSTD[:, 0:n], ALU.mult),
                     reads=list(src_bufs) + [RSTD_B], writes=[TMP_BS[k]])
                if c % 2 == 0:
                    P.op("act", act_fn(HM[:, c, 0:n], TMPS[k][:, 0:n], AF.Identity,
                                       bias=DER[:, l, jb, q, c:c + 1], scale=DER[:, l, ja, q, c:c + 1]),
                         reads=[TMP_BS[k], B_DER], writes=[HM_B])
                else:
                    P.op("dve", ts(HM[:, c, 0:n], TMPS[k][:, 0:n], DER[:, l, ja, q, c:c + 1], DER[:, l, jb, q, c:c + 1],
                                   ALU.mult, ALU.add), reads=[TMP_BS[k], B_DER], writes=[HM_B])

        def post_update(ychunks, y_bufs, c0, n, l, which, q, SQ, SQ_B, RSTD, RSTD_B, TMPS, TMP_BS, pb, sq_done=False, y_all=None):
            jg = 2 if which == 1 else 5
            if y_all is not None:
                P.op("act", act_fn(SQ[:, :, 0:n], y_all, AF.Square), reads=y_bufs, writes=[SQ_B])
            elif not sq_done:
                for c in range(8):
                    P.op("act", act_fn(SQ[:, c, 0:n], ychunks[c], AF.Square), reads=y_bufs, writes=[SQ_B])
            P.pe([mm(psum[pb][:, 0:n], ONESB, SQ[:, c, 0:n], c == 0, c == 7) for c in range(8)],
                 reads=[SQ_B, B_CST], writes=[PSB[pb]])
            rstd_from(psum[pb][:, 0:n], n, 1.0 / D, TMPS[0], TMP_BS[0], RSTD, RSTD_B, PSB[pb])
            xb = xbufs(c0, n)

            def upd(c):
                k = c % 2
                P.op("dve", stt(XT[:, c, c0:c0 + n], TMPS[k][:, 0:n], DER[:, l, jg, q, c:c + 1], XT[:, c, c0:c0 + n],
                                ALU.mult, ALU.add),
                     reads=[TMP_BS[k], B_DER] + xb, writes=xb)
            for c in range(8):
                k = c % 2
                P.op("dve", tt(TMPS[k][:, 0:n], ychunks[c], RSTD[:, 0:n], ALU.mult),
                     reads=list(y_bufs) + [RSTD_B], writes=[TMP_BS[k]])
                if c > 0:
                    upd(c - 1)
            upd(7)

        def dump_x(tag):
            if debug_phase == tag:
                dd = P.new_dsem()
                P.dma("sp", lambda e: e.dma_start(out=dbg_d.ap(), in_=XT[:]), dd, reads=XBUF)
            if stop_after == tag and not P.stopped:
                P.barrier()
                P.stopped = True

        P.barrier()
        dump_x("load")

        SC.reset()
        MODE["lnexp"] = True
        HMS = [SC.bf16(8 * 128).rearrange("p (c n) -> p c n", c=8) for _ in range(2)]
        HM = HMS[0]
        SQ = HM
        RSTD = SC.f32(128)
        TMPS = [SC.f32(128), SC.f32(128)]
        U = SC.bf16(4 * 128).rearrange("p (c n) -> p c n", c=4)
        SG = SC.bf16(4 * 128).rearrange("p (c n) -> p c n", c=4)
        QF = SC.f32(2 * 128).rearrange("p (c n) -> p c n", c=2)
        KF_ = SC.f32(2 * 128).rearrange("p (c n) -> p c n", c=2)
        LR = SC.bf16(128)
        CAT = SC.bf16(8 * 128).rearrange("p (c n) -> p c n", c=8)
        VG = SC.f32(512)
        VTM = SC.bf16(512)
        VVTM = SC.bf16(512)
        KTM = SC.f32(256)
        LA = SC.f32(512)
        ED = SC.f32(256)
        KD = SC.bf16(256)
        EB = SC.f32(512)
        ENB = SC.f32(512)
        QEZ = SC.bf16(1024).rearrange("p (r h n) -> p r h n", r=2, h=4)
        KE = SC.bf16(512).rearrange("p (r h n) -> p r h n", r=2, h=2)
        P2REG = SC.f32(2048)
        ATT = P2REG[:, 0:512].bitcast(BF16).rearrange("p (r h n) -> p r h n", r=2, h=4)
        SIN16 = P2REG[:, 512:768].bitcast(BF16).rearrange("p (r h n) -> p r h n", r=2, h=2)
        OSQ = P2REG[:, 768:1024].bitcast(BF16)
        ORS = P2REG[:, 1024:1536]
        OT = P2REG[:, 1536:2048]
        MW1 = P2REG.bitcast(BF16).rearrange("p (k n) -> p k n", k=8)
        MW1_B, MW1_S = Buf("mw1"), P.new_dsem()
        DECF = SC.f32(4)
        LNST = SC.f32(8)
        LNMV = SC.f32(4)
        DSB = SC.bf16(NTILE * 256).rearrange("p (t h n) -> p t h n", t=NTILE, h=2)
        GST = SC.f32(7 * 256).rearrange("p (s h n) -> p s h n", s=7, h=2)
        NSG = SC.f32(8 * 128).rearrange("p (s r h n) -> p s r h n", s=2, r=2, h=2)
        GDEC = SC.f32(NTILE * 2).rearrange("p (t h) -> p t h", h=2)
        GPB = SC.f32(NTILE * 2).rearrange("p (t h) -> p t h", h=2)
        assert SC.off <= NSC, SC.off
        print("L0 scratch cols", SC.off, "of", NSC)
        (B_HM, B_RSTD, B_U, B_SG, B_QF, B_KF, B_LR, B_CAT, B_VG, B_VTM, B_VVTM, B_KTM, B_LA, B_ED,
         B_KD, B_EB, B_ENB, B_QE, B_KE, B_ATT, B_SIN, B_OSQ, B_ORS, B_OT, B_DECF, B_LN) = [Buf("l0_%d" % i) for i in range(26)]
        B_HMS = [B_HM, Buf("l0_hm1")]
        B_SQ = B_HM
        B_TMPS = [Buf("tmp0"), Buf("tmp1")]
        DSB_B = [Buf("dsb%d" % t) for t in range(NTILE)]
        B_GD = Buf("gdec")
        SF, RB0, RB1, RECVF, RECVB, SFI = (GST[:, i] for i in range(6))
        B_SF, B_SFI, B_RECV, B_NSGF, B_NSGB = Buf("sf"), Buf("sfi"), Buf("recv"), Buf("nsgf"), Buf("nsgb")

        def cond_of_tile(t):
            return 0 if t < 4 else 1

        def l0_block_norm(t):
            c0 = t * 128
            hm, b_hm = HMS[t % 2], B_HMS[t % 2]
            norm_mod(xcols(c0, 128), xbufs(c0, 128), 128, 0, 1, cond_of_tile(t), hm, b_hm, hm, b_hm, RSTD, B_RSTD,
                     TMPS, B_TMPS, 0)

        def fm_proj(col0, m, pb, n=128):
            P.pe([mm(psum[pb][0:m, 0:n], W_IN0[:, k, col0:col0 + m], HM[:, k, 0:n], k == 0, k == 7)
                  for k in range(8)], reads=[B_HM] + win0_bufs(col0, col0 + m), writes=[PSB[pb]])

        def l0_lr():
            fm_proj(2560, 32, 1)
            P.op("act", act_fn(LR[0:32, :], psum[1][0:32, 0:128], AF.Copy), reads=[PSB[1]], writes=[B_LR])

        def l0_tile_tm(want_v):
            P.pe([mm(psum[2][:, 0:256], HM[:, k, :], W_IN0[:, k, 1280:1536], k == 0, k == 7) for k in range(8)],
                 reads=[B_HM] + win0_bufs(1280, 1536), writes=[PSB[2]])
            P.pe([mm(psum[3][:, 0:512], HM[:, k, :], W_IN0[:, k, 1536:2048], k == 0, k == 7) for k in range(8)],
                 reads=[B_HM] + win0_bufs(1536, 2048), writes=[PSB[3]])
            P.op("act", act_fn(KTM, psum[2][:, 0:256], AF.Copy), reads=[PSB[2]], writes=[B_KTM])
            P.op("dve", cp(VVTM, psum[3][:, 0:512]), reads=[PSB[3]], writes=[B_VVTM])
            P.pe([mm(psum[4][:, 0:512], LR[0:32, :], GWBD[:], True, False),
                  mm(psum[4][:, 0:512], ONESB[0:1, :], GBROW[:], False, True)],
                 reads=[B_LR, B_MISC, B_CST], writes=[PSB[4]])
            P.op("act", act_fn(LA, psum[4][:, 0:512], AF.Exp, scale=-1.0), reads=[PSB[4]], writes=[B_LA])
            P.op("act", act_fn(LA, LA, AF.Ln, bias=1.0), reads=[B_LA], writes=[B_LA])
            if want_v:
                P.pe([mm(psum[5][:, 0:512], HM[:, k, :], W_IN0[:, k, 512:1024], k == 0, k == 7) for k in range(8)],
                     reads=[B_HM] + win0_bufs(512, 1024), writes=[PSB[5]])

        def l0_kd_ds(r, pb_e, pb_ds):
            tri = TRI_SF if r == 0 else TRI_SB
            P.pe([mm(psum[pb_e][:, 0:256], tri, LA[:, r * 256:(r + 1) * 256], True, True)],
                 reads=[B_LA, B_CST], writes=[PSB[pb_e]])
            P.op("act", act_fn(ED, psum[pb_e][:, 0:256], AF.Exp), reads=[PSB[pb_e]], writes=[B_ED])
            P.op("dve", tt(KD, KTM, ED, ALU.mult), reads=[B_KTM, B_ED], writes=[B_KD])
            fns = []
            for h in range(4):
                hp, hh = h // 2, h % 2
                fns.append(mm(psum[pb_ds][hh * 64:(hh + 1) * 64, hp * 128:(hp + 1) * 128], KD[:, h * 64:(h + 1) * 64],
                              VVTM[:, h * 128:(h + 1) * 128], True, True))
            P.pe(fns, reads=[B_KD, B_VVTM], writes=[PSB[pb_ds]])

        def l0_dec(r, dst, dst_b, pb):
            fns = [mm(psum[pb][:, 256 + 2 * hp:256 + 2 * hp + 2], LA[:, r * 256 + hp * 128:r * 256 + (hp + 1) * 128], NEGC[:, 0:2], True, True)
                   for hp in range(2)]
            P.pe(fns, reads=[B_LA, B_CST], writes=[PSB[pb]])
            P.op("act", act_fn(dst, psum[pb][:, 256:260].rearrange("p (h two) -> p h two", two=2)[:, :, 0], AF.Exp),
                 reads=[PSB[pb]], writes=[dst_b])

        def gla_state_step(S, S_B, dec, dec_b, ds_ap, ds_b):
            for hp in range(2):
                P.op("dve", stt(S[:, hp, :], S[:, hp, :], dec[:, hp:hp + 1], ds_ap[:, hp * 128:(hp + 1) * 128], ALU.mult, ALU.add),
                     reads=[S_B, dec_b, ds_b], writes=[S_B])

        dns_f = P.new_dsem()
        dns_b = P.new_dsem()
        l0_block_norm(0)
        for t in range(NTILE):
            HM, B_HM = HMS[t % 2], B_HMS[t % 2]
            l0_lr()
            if t == 2 or t == 0:
                P.op("dve", lambda e: e.memset(SF[:], 0.0), writes=[B_SF])
            if t == 4:
                P.op("dve", cp(SF[:], GINIT[:, 0]), reads=[B_MISC], writes=[B_SF])
            l0_tile_tm(False)
            if t + 1 < NTILE:
                l0_block_norm(t + 1)
            l0_kd_ds(0, 5, 6)
            l0_dec(0, DECF[:, 0:2], B_DECF, 5)
            gla_state_step(SF, B_SF, DECF, B_DECF, psum[6][:, 0:256], PSB[6])
            if t in (1, 3):
                P.op("act", act_fn(NSG[:, t // 2, 0], SF[:], AF.Copy), reads=[B_SF], writes=[B_NSGF])
            l0_kd_ds(1, 5, 7)
            l0_dec(1, GDEC[:, t, :], B_GD, 5)
            P.op("act", act_fn(DSB[:, t].rearrange("p h n -> p (h n)"), psum[7][:, 0:256], AF.Copy),
                 reads=[PSB[7]], writes=[DSB_B[t]])
            if t < 12:
                mod_piece(1, t, MW1, MW1_B, MW1_S, 1)
        derive(1)
        P.barrier()
        dump_x("l0p1")
        RBS = [RB0, RB1]
        B_RBS = [Buf("rb0"), Buf("rb1")]
        B_GPB = Buf("gpb")
        PRUN = [GST[:, 6, 0, 0:2], GST[:, 6, 1, 0:2]]
        B_PRUN = [Buf("prun0"), Buf("prun1")]

        def bwd_scan(tiles, init_ap, track_p):
            cur = 0
            if init_ap is None:
                P.op("dve", lambda e: e.memset(RBS[0][:], 0.0), writes=[B_RBS[0]])
            else:
                P.op("dve", cp(RBS[0][:], init_ap), reads=[B_MISC], writes=[B_RBS[0]])
            if track_p:
                P.op("dve", lambda e: e.memset(PRUN[0], 1.0), writes=[B_PRUN[0]])
            pc_ = 0
            for t in reversed(tiles):
                nxt = 1 - cur
                for hp in range(2):
                    P.op("dve", stt(RBS[nxt][:, hp, :], RBS[cur][:, hp, :], GDEC[:, t, hp:hp + 1], DSB[:, t, hp, :], ALU.mult, ALU.add),
                         reads=[B_RBS[cur], B_GD, DSB_B[t]], writes=[B_RBS[nxt]])
                P.op("act", act_fn(DSB[:, t], RBS[cur][:], AF.Copy), reads=[B_RBS[cur]], writes=[DSB_B[t]])
                if track_p:
                    P.op("dve", cp(GPB[:, t, :], PRUN[pc_]), reads=[B_PRUN[pc_]], writes=[B_GPB])
                    P.op("dve", tt(PRUN[1 - pc_], PRUN[pc_], GDEC[:, t, :], ALU.mult), reads=[B_PRUN[pc_], B_GD], writes=[B_PRUN[1 - pc_]])
                    pc_ = 1 - pc_
                cur = nxt
            return cur

        for s in range(2):
            cur = bwd_scan([2 * s, 2 * s + 1], None, False)
            P.op("act", act_fn(NSG[:, s, 1], RBS[cur][:], AF.Copy), reads=[B_RBS[cur]], writes=[B_NSGB])
        cur = bwd_scan(list(range(4, NTILE)), GINIT[:, 1], True)
        dcc = P.new_dsem()
        B_CC1a, B_CC1b = Buf("cc1a"), Buf("cc1b")
        P.dma("sp", lambda e: e.dma_start(out=cc1_in.ap()[0:128, :], in_=SF[:].rearrange("p h n -> p (h n)")), dcc, reads=[B_SF], writes=[B_CC1a])
        dcc2 = P.new_dsem()
        P.dma("sp", lambda e: e.dma_start(out=cc1_in.ap()[128:256, :], in_=RBS[cur][:].rearrange("p h n -> p (h n)")), dcc2,
              reads=[B_RBS[cur]], writes=[B_CC1b])
        cc_count = [0]

        def exchange(src, dst, b_ins, b_out):
            if P.stopped:
                return
            st = P.st["pool"]
            for tkn in P._deps(b_ins, [b_out]):
                P._wait(st, tkn)
            cc_count[0] += 1
            v = cc_count[0]
            st.ops.append(lambda e: e.collective_compute("AllGather", ALU.bypass, replica_groups=PAIRS,
                                                         ins=[src.ap().opt()], outs=[dst.ap().opt()]).then_inc(cc_sem, 1))
            tok = ("cc", cc_sem, v)
            P._mark(tok, b_ins, [b_out])

        B_CC1O = Buf("cc1o")
        exchange(cc1_in, cc1_out, [B_CC1a, B_CC1b], B_CC1O)
        dcc3 = P.new_dsem()
        B_RECVF, B_RECVB = Buf("recvf"), Buf("recvb")
        P.dma("sp", lambda e: e.dma_start(out=RECVF[:].rearrange("p h n -> p (h n)"), in_=cc1_out.ap()[0:128, :]), dcc3,
              reads=[B_CC1O], writes=[B_RECVF])
        dcc4 = P.new_dsem()
        P.dma("sp", lambda e: e.dma_start(out=RECVB[:].rearrange("p h n -> p (h n)"), in_=cc1_out.ap()[384:512, :]), dcc4,
              reads=[B_CC1O], writes=[B_RECVB])
        P.op("dve", stt(SFI[:], RECVF[:], FLG[:, 1:2], GINIT[:, 0], ALU.mult, ALU.add), reads=[B_RECVF, B_MISC], writes=[B_SFI])
        P.op("dve", ts(RECVB[:], RECVB[:], FLG[:, 0:1], None, ALU.mult), reads=[B_RECVB, B_MISC], writes=[B_RECVB])
        for t in range(4, NTILE):
            for hp in range(2):
                P.op("dve", stt(DSB[:, t, hp, :], RECVB[:, hp, :], GPB[:, t, hp:hp + 1], DSB[:, t, hp, :], ALU.mult, ALU.add),
                     reads=[B_RECVB, B_GPB, DSB_B[t]], writes=[DSB_B[t]])
        P.dma("sp", lambda e: e.dma_start(out=nsg_d.ap().rearrange("s r (hp hh) d v -> (hh d) s r hp v", hh=2), in_=NSG[:]),
              dns_f, reads=[B_NSGF, B_NSGB])

        P.barrier()
        dump_x("l0ex")

        def out_proj_and_update(src, t, l, q):
            WS0.project(CAT, B_CAT, 128, lambda oc: psum[oc // 4][:, (oc % 4) * 128:(oc % 4) * 128 + 128],
                        lambda oc: PSB[oc // 4], t == NTILE - 1)
            ych = [psum[c // 4][:, (c % 4) * 128:(c % 4) * 128 + 128] for c in range(8)]
            post_update(ych, PSB[0:2], t * 128, 128, l, 1, q, SQ, B_SQ, RSTD, B_RSTD, TMPS, B_TMPS, 4,
                        y_all=PS[:, 0:1024].rearrange("p (c n) -> p c n", c=8))

        WS0 = WStream(WOUT_OFF, ev_w_out)
        WS0.prefetch()
        P.op("dve", lambda e: e.memset(QEZ[:], 0.0), writes=[B_QE])
        HGN = GST[:, 6, 1, 8:12]
        B_HGN = Buf("hgn")
        P.op("dve", ts(HGN, PCOL[:, 2, 0:4], 0.5, None, ALU.mult), reads=[B_PCOL], writes=[B_HGN])
        l0_block_norm(0)
        for t in range(NTILE):
            q = cond_of_tile(t)
            HM, B_HM = HMS[t % 2], B_HMS[t % 2]
            SQ, B_SQ = HM, B_HM
            l0_lr()
            if t in (0, 2):
                P.op("dve", lambda e: e.memset(SF[:], 0.0), writes=[B_SF])
            if t == 4:
                P.op("dve", cp(SF[:], SFI[:]), reads=[B_SFI], writes=[B_SF])
            l0_tile_tm(True)
            for cc in range(4):
                fm_proj(cc * 128, 128, 1)
                P.op("act", act_fn(U[:, cc, :], psum[1][:, 0:128], AF.Gelu_apprx_tanh), reads=[PSB[1]], writes=[B_U])
            for cc in range(4):
                fm_proj(2048 + cc * 128, 128, 2)
                P.op("act", act_fn(TMPS[0], psum[2][:, 0:128], AF.Tanh, scale=0.5), reads=[PSB[2]], writes=[B_TMPS[0]])
                P.op("dve", stt(TMPS[1], TMPS[0], 1.0, psum[2][:, 0:128], ALU.add, ALU.mult), reads=[B_TMPS[0], PSB[2]], writes=[B_TMPS[1]])
                P.op("dve", ts(SG[:, cc, :], TMPS[1], HGN[:, cc:cc + 1], None, ALU.mult), reads=[B_TMPS[1], B_HGN], writes=[B_SG])
            P.op("act", act_fn(VG, psum[5][:, 0:512], AF.Gelu_apprx_tanh), reads=[PSB[5]], writes=[B_VG])
            for cc in range(2):
                fm_proj(1024 + cc * 128, 128, 1)
                P.op("act", act_fn(QF[:, cc, :], psum[1][:, 0:128], AF.Copy, scale=0.125), reads=[PSB[1]], writes=[B_QF])
                fm_proj(1280 + cc * 128, 128, 2)
                P.op("dve", cp(KF_[:, cc, :], psum[2][:, 0:128]), reads=[PSB[2]], writes=[B_KF])
            if t + 1 < NTILE:
                l0_block_norm(t + 1)
            P.op("dve", lambda e: e.bn_stats(out=LNST[:, 0:6], in_=VG), reads=[B_VG], writes=[B_LN])
            P.op("dve", lambda e: e.bn_aggr(out=LNMV[:, 0:2], in_=LNST[:, 0:6]), reads=[B_LN], writes=[B_LN])
            P.op("act", act_fn(LNMV[:, 2:3], LNMV[:, 1:2], AF.Ln, bias=EPS), reads=[B_LN], writes=[B_LN])
            P.op("act", act_fn(LNMV[:, 3:4], LNMV[:, 2:3], AF.Exp, scale=-0.5), reads=[B_LN], writes=[B_LN])
            P.op("dve", ts(VG, VG, LNMV[:, 0:1], LNMV[:, 3:4], ALU.subtract, ALU.mult), reads=[B_VG, B_LN], writes=[B_VG])
            P.op("dve", tt(VG, VG, LNG[:], ALU.mult), reads=[B_VG, B_MISC], writes=[B_VG])
            P.op("dve", tt(VTM, VG, LNB[:], ALU.add), reads=[B_VG, B_MISC], writes=[B_VTM])
            fns = []
            for g in range(4):
                fns.append(mm(psum[5][:, g * 128:(g + 1) * 128], VTM[:, g * 128:(g + 1) * 128], WST[:, g, :], True, False))
                fns.append(mm(psum[5][:, g * 128:(g + 1) * 128], ONESB[0:1, :], BSROW[:, g * 128:(g + 1) * 128], False, True))
            P.pe(fns, reads=[B_VTM, B_MISC, B_CST], writes=[PSB[5]])
            P.op("dve", tt(CAT[:, 0:4, :], psum[5][:].rearrange("p (g n) -> p g n", g=4), U[:], ALU.mult),
                 reads=[PSB[5], B_U], writes=[B_CAT])
            fns = []
            for r in range(2):
                tri = TRI_IF if r == 0 else TRI_IB
                for hp in range(2):
                    fns.append(mm(psum[6][:, (r * 2 + hp) * 128:(r * 2 + hp + 1) * 128],
                                  LA[:, r * 256 + hp * 128:r * 256 + (hp + 1) * 128], tri, True, True))
            P.pe(fns, reads=[B_LA, B_CST], writes=[PSB[6]])
            P.op("act", act_fn(EB, psum[6][:, 0:512], AF.Exp), reads=[PSB[6]], writes=[B_EB])
            P.op("act", act_fn(ENB, psum[6][:, 0:512], AF.Exp, scale=-1.0), reads=[PSB[6]], writes=[B_ENB])
            for r in range(2):
                for hh in range(2):
                    ps_ = slice(hh * 64, (hh + 1) * 64)
                    P.op("dve", tt(QEZ[ps_, r, hh::2, :], QF[ps_, :, :],
                                   EB[ps_, r * 256:(r + 1) * 256].rearrange("p (h n) -> p h n", h=2), ALU.mult),
                         reads=[B_QF, B_EB], writes=[B_QE])
                P.op("dve", tt(KE[:, r], KF_[:], ENB[:, r * 256:(r + 1) * 256].rearrange("p (h n) -> p h n", h=2), ALU.mult),
                     reads=[B_KF, B_ENB], writes=[B_KE])
            for r in range(2):
                fns = []
                for h in range(4):
                    hp = h // 2
                    fns.append(mm(psum[r][:, h * 128:(h + 1) * 128], KE[:, r, hp, :], QEZ[:, r, h, :], True, True))
                P.pe(fns, reads=[B_QE, B_KE], writes=[PSB[r]])
                P.op("dve", tt(ATT[:, r], psum[r][:].rearrange("p (h n) -> p h n", h=4),
                               MASKB16[r].unsqueeze(1).to_broadcast([128, 4, 128]), ALU.mult),
                     reads=[PSB[r], B_CST], writes=[B_ATT])
            P.op("act", act_fn(SIN16[:, 0], SF[:], AF.Copy), reads=[B_SF], writes=[B_SIN])
            P.op("act", act_fn(SIN16[:, 1], DSB[:, t], AF.Copy), reads=[DSB_B[t]], writes=[B_SIN])
            fns = []
            for h in range(4):
                hp, hh = h // 2, h % 2
                o_ap = psum[7][:, h * 128:(h + 1) * 128]
                fns.append(mm(o_ap, VVTM[:, h * 128:(h + 1) * 128], ATT[:, 0, h, :], True, False))
                fns.append(mm(o_ap, VVTM[:, h * 128:(h + 1) * 128], ATT[:, 1, h, :], False, False))
                fns.append(mm(o_ap, SIN16[:, 0, hp, :], QEZ[:, 0, h, :], False, False))
                fns.append(mm(o_ap, SIN16[:, 1, hp, :], QEZ[:, 1, h, :], False, True))
            P.pe(fns, reads=[B_VVTM, B_ATT, B_SIN, B_QE], writes=[PSB[7]])
            l0_kd_ds(0, 2, 3)
            l0_dec(0, DECF[:, 0:2], B_DECF, 2)
            gla_state_step(SF, B_SF, DECF, B_DECF, psum[3][:, 0:256], PSB[3])
            P.op("act", act_fn(OSQ, psum[7][:, 0:512], AF.Square), reads=[PSB[7]], writes=[B_OSQ])
            P.pe([mm(psum[4][:, 0:512], ONESB, OSQ, True, True)], reads=[B_OSQ, B_CST], writes=[PSB[4]])
            P.op("act", act_fn(OT, psum[4][:, 0:512], AF.Ln, bias=EPS, scale=1.0 / 128), reads=[PSB[4]], writes=[B_OT])
            P.op("act", act_fn(ORS, OT, AF.Exp, scale=-0.5), reads=[B_OT], writes=[B_ORS])
            P.op("dve", tt(OT, psum[7][:, 0:512], ORS, ALU.mult), reads=[PSB[7], B_ORS], writes=[B_OT])
            P.op("dve", tt(CAT[:, 4:8, :], OT.rearrange("p (h n) -> p h n", h=4), SG[:], ALU.mult),
                 reads=[B_OT, B_SG], writes=[B_CAT])
            out_proj_and_update(ev_w_out, t, 0, q)
        MODE["lnexp"] = False
        P.barrier()
        dump_x("l0mix")

        NSLOT = 6
        SLOT = [wslot(i * 4096, 8, 512) for i in range(NSLOT)]
        SLOT_B = [Buf("slot%d" % i) for i in range(NSLOT)]
        SLOT_S = [P.new_dsem() for _ in range(NSLOT)]
        SLOT_HS = [P.new_dsem() for _ in range(NSLOT)]

        def mlp_layer(l, final):
            SC.reset()
            MODE["lnexp"] = True
            HM5S = [SC.bf16(8 * 512).rearrange("p (c n) -> p c n", c=8) for _ in range(2)]
            H1f = SC.f32(16 * 256)
            H1 = H1f.bitcast(BF16).rearrange("p (c n) -> p c n", c=16)
            YS = SC.f32(8 * 512).rearrange("p (c n) -> p c n", c=8)
            RS5 = SC.f32(512)
            TM5 = [SC.f32(512), SC.f32(512)]
            ZB = [SC.f32(512), SC.f32(512)]
            OST = [H1f[:, 0:1024], H1f[:, 1024:2048]]
            assert SC.off <= NSC, SC.off
            B_HM5S, B_YS, B_RS5 = [Buf("hm5a"), Buf("hm5b")], Buf("ys"), Buf("rs5")
            B_H1 = [Buf("h1_%d" % i) for i in range(16)]
            B_TM5 = [Buf("tm5a"), Buf("tm5b")]
            B_ZB = [Buf("zba"), Buf("zbb")]
            OST_S = [P.new_dsem(), P.new_dsem()] if final else None
            pieces = []
            for sbk in range(NSB):
                for hf in range(2):
                    for i in range(4):
                        pieces.append(("w1", sbk, hf, i))
                    for i in range(4):
                        pieces.append(("w2", sbk, hf, i))

            B_WC = [Buf("wc%d" % i) for i in range(16)]
            WC_S = [P.new_dsem() for _ in range(4)]

            def load_piece(idx):
                kind, sbk, hf, i = pieces[idx]
                s = idx % NSLOT
                pid = hf * 8 + (0 if kind == "w1" else 4) + i
                flat = SLOT[s].rearrange("p k n -> p (k n)")
                if sbk > 0:
                    P.dma("sp", lambda e: e.dma_start(out=flat, in_=wcache.ap()[pid]), SLOT_HS[s], reads=[B_WC[pid]], writes=[SLOT_B[s]])
                    return
                if kind == "w1":
                    c0 = hf * 2048 + i * 512
                    load_w_rows(SLOT[s], mlp_w1.ap()[l], c0, 512, SLOT_S[s], SLOT_B[s])
                else:
                    dst = flat.rearrange("p (k n) -> p k n", k=16)
                    src = mlp_w2.ap()[l][hf * 2048:(hf + 1) * 2048, i * 256:(i + 1) * 256].rearrange("(k p) n -> p k n", p=128)
                    P.dma("pool", lambda e: e.dma_start(out=dst, in_=src), SLOT_S[s], writes=[SLOT_B[s]])
                P.dma("sp", lambda e: e.dma_start(out=wcache.ap()[pid], in_=flat), WC_S[pid % 4], reads=[SLOT_B[s]], writes=[B_WC[pid]])

            for idx in range(NSLOT - 1):
                load_piece(idx)
            nxt = NSLOT - 1
            idx = 0
            bank_rr = [0]
            def mlp_norm(sbk):
                qq = 0 if sbk == 0 else 1
                hb = sbk % 2
                norm_mod(xcols(sbk * 512, 512), xbufs(sbk * 512, 512), 512, l, 2, qq, HM5S[hb], B_HM5S[hb], HM5S[hb], B_HM5S[hb],
                         RS5, B_RS5, TM5, B_TM5, 7)

            mlp_norm(0)
            for sbk in range(NSB):
                q = 0 if sbk == 0 else 1
                c0 = sbk * 512
                HM5, B_HM5 = HM5S[sbk % 2], B_HM5S[sbk % 2]
                SQ5, B_SQ5 = HM5, B_HM5
                for hf in range(2):
                    for i in range(4):
                        s = idx % NSLOT
                        for cc in range(4):
                            hc = i * 4 + cc
                            bank = bank_rr[0] % 4
                            bank_rr[0] += 1
                            P.pe([mm(psum[bank][:, 0:512], SLOT[s][:, k, cc * 128:(cc + 1) * 128], HM5[:, k, :], k == 0, k == 7)
                                  for k in range(8)], reads=[SLOT_B[s], B_HM5], writes=[PSB[bank]])
                            z = hc % 2
                            P.op("act", act_fn(ZB[z], psum[bank][:, 0:512], AF.Identity,
                                               bias=pcol(1, 64 + l * 32 + hf * 16 + hc)),
                                 reads=[PSB[bank], B_PCOL], writes=[B_ZB[z]])
                            P.op("dve", stt(H1[:, hc, :], ZB[z], 0.0, ZB[z], ALU.max, ALU.mult), reads=[B_ZB[z]], writes=[B_H1[hc]])
                        idx += 1
                        if nxt < len(pieces):
                            load_piece(nxt)
                            nxt += 1
                    if hf == 1 and sbk + 1 < NSB:
                        mlp_norm(sbk + 1)
                    for i in range(4):
                        s = idx % NSLOT
                        w2v = SLOT[s].rearrange("p k n -> p (k n)").rearrange("p (k n) -> p k n", k=16)
                        for cc in range(2):
                            oc = i * 2 + cc
                            bank = 4 + (oc % 2)
                            P.pe([mm(psum[bank][:, 0:512], w2v[:, k, cc * 128:(cc + 1) * 128], H1[:, k, :], k == 0, k == 15)
                                  for k in range(16)], reads=[SLOT_B[s]] + B_H1, writes=[PSB[bank]])
                            if hf == 0:
                                P.op("act", act_fn(YS[:, oc, :], psum[bank][:, 0:512], AF.Identity, bias=pcol(0, 96 + l * 8 + oc)),
                                     reads=[PSB[bank], B_PCOL], writes=[B_YS])
                            else:
                                P.op("dve", tt(YS[:, oc, :], YS[:, oc, :], psum[bank][:, 0:512], ALU.add),
                                     reads=[PSB[bank], B_YS], writes=[B_YS])
                                P.op("act", act_fn(SQ5[:, oc, :], YS[:, oc, :], AF.Square), reads=[B_YS], writes=[B_SQ5])
                        idx += 1
                        if nxt < len(pieces):
                            load_piece(nxt)
                            nxt += 1
                post_update([YS[:, c, :] for c in range(8)], [B_YS], c0, 512, l, 2, q, SQ5, B_SQ5, RS5, B_RS5, TM5, B_TM5, 7,
                            sq_done=True)
                if final:
                    for ti in range(4):
                        t = sbk * 4 + ti
                        o = t % 2
                        for half in range(2):
                            pb = 6 if half == 0 else 7
                            P.pe([lambda e, c=c, pb=pb, half=half, t=t: e.transpose(
                                psum[pb][:, (c - 4 * half) * 128:(c - 4 * half + 1) * 128], XT[:, c, t * 128:(t + 1) * 128], IDENT)
                                for c in range(4 * half, 4 * half + 4)], reads=[XBUF[t], B_CST], writes=[PSB[pb]])
                            ob = B_H1[o * 4 + half * 2:o * 4 + half * 2 + 2]
                            if half == 0:
                                P.op("act", act_fn(OST[o][:, 0:512], psum[pb][:, 0:512], AF.Copy), reads=[PSB[pb]], writes=ob)
                            else:
                                P.op("dve", cp(OST[o][:, 512:1024], psum[pb][:, 0:512]), reads=[PSB[pb]], writes=ob)
                        P.dma("sp", lambda e, t=t, o=o: e.dma_start(out=y_d.ap()[t * 128:(t + 1) * 128, :], in_=OST[o]),
                              OST_S[o], reads=B_H1[o * 4:o * 4 + 4])

        mlp_layer(0, False)
        MODE["lnexp"] = False
        P.barrier()
        dump_x("l0mlp")

        W_IN1 = wslot(0, 8, 2048)
        WIN1_B = [Buf("win1_%d" % i) for i in range(4)]
        WIN1_S = [P.new_dsem() for _ in range(4)]
        for i in range(4):
            load_w_rows(W_IN1[:, :, i * 512:(i + 1) * 512], rg_w_in.ap(), i * 512, 512, WIN1_S[i], WIN1_B[i])
        WOUT1_OFF = 8 * 2048
        W_OUTS1 = wslot(WOUT1_OFF, 8, 512)
        GOFF = WOUT1_OFF + 4096
        WGATE = WBt[:, GOFF:GOFF + 4096].rearrange("p (a r c n) -> p a r c n", a=2, r=2, c=8)
        B_WG = Buf("wgate")
        P.op("pool", lambda e: e.memset(WBt[:, GOFF:GOFF + 4096], 0.0), writes=[B_WG])
        for a, src in enumerate((rg_wa, rg_wx)):
            for r in range(2):
                for par in range(2):
                    dsg = P.new_dsem()
                    P.dma("pool", lambda e, a=a, r=r, par=par, src=src: e.dma_start(
                        out=WGATE[par * 64:(par + 1) * 64, a, r, :, par * 64:(par + 1) * 64],
                        in_=src.ap()[r].rearrange("(c two) i j -> two i c j", two=2)[par]), dsg, writes=[B_WG])

        SC.reset()
        HMS1 = [SC.bf16(8 * 256).rearrange("p (c n) -> p c n", c=8) for _ in range(2)]
        HM = HMS1[0]
        SQ = HM
        RSTD = SC.f32(256)
        TMPS = [SC.f32(256), SC.f32(256)]
        CAT = SC.bf16(8 * 256).rearrange("p (c n) -> p c n", c=8)
        XE = SC.f32(8 * 24).rearrange("p (c n) -> p c n", c=8)
        HME = SC.bf16(8 * 24).rearrange("p (c n) -> p c n", c=8)
        EDGE = SC.f32(8 * 24).rearrange("p (c n) -> p c n", c=8)
        HALO = SC.f32(8 * 3).rearrange("p (c n) -> p c n", c=8)
        XBH = [SC.f32(260), SC.f32(260)]
        XC = [SC.f32(256), SC.f32(256)]
        XCB = SC.bf16(8 * 256).rearrange("p (c n) -> p c n", c=8)
        GG = SC.bf16(8 * 256).rearrange("p (c n) -> p c n", c=8)
        THR = [[SC.f32(512).rearrange("p (c n) -> p c n", c=2) for _ in range(2)] for _ in range(2)]
        A2 = [[SC.f32(512).rearrange("p (c n) -> p c n", c=2) for _ in range(2)] for _ in range(2)]
        THI = [[SC.f32(512).rearrange("p (c n) -> p c n", c=2) for _ in range(2)] for _ in range(2)]
        HF = [SC.f32(512).rearrange("p (c n) -> p c n", c=2) for _ in range(2)]
        HBIAS = SC.f32(32).rearrange("p (a r c) -> p a r c", a=2, r=2)
        CLH = SC.f32(16).rearrange("p (r c) -> p r c", r=2)
        RACC = SC.f32(4)
        RUNF = SC.f32(8)
        RUNB = SC.f32(8)
        RUNP = SC.f32(8)
        RCV = SC.f32(16)
        PBLK = SC.f32(NMB * 8).rearrange("p (m c) -> p m c", c=8)
        FBLK = SC.f32(NMB * 8).rearrange("p (m c) -> p m c", c=8)
        SINB = SC.f32(NMB * 8).rearrange("p (m c) -> p m c", c=8)
        CPB = SC.f32(NMB * 8).rearrange("p (m c) -> p m c", c=8)
        NSR = SC.f32(32).rearrange("p (s r c) -> p s r c", s=2, r=2)
        NSRT = SC.f32(128)
        SEND = SC.f32(24).rearrange("p (c n) -> p c n", c=8)
        RC2 = SC.f32(48).rearrange("p (k c n) -> p k c n", k=2, c=8)
        SEND3 = SC.f32(16)
        RC3 = SC.f32(32).rearrange("p (k n) -> p k n", k=2)
        assert SC.off <= NSC, SC.off
        print("L1 scratch cols", SC.off, "of", NSC)
        (B_HM, B_RSTD, B_CAT, B_XE, B_HME, B_EDGE, B_HALO, B_XCB, B_GG, B_HF,
         B_RACC, B_RUNF, B_RUNB, B_RUNP, B_RCV, B_BLK, B_NSR, B_BROW, B_L1C) = [Buf("l1_%d" % i) for i in range(19)]
        B_TMPS = [Buf("tmp0"), Buf("tmp1")]
        B_XBH = [Buf("xbh0"), Buf("xbh1")]
        B_XC = [Buf("xc0"), Buf("xc1")]
        B_THR = [[Buf("thr"), Buf("thr")] for _ in range(2)]
        B_A2 = [[Buf("a2"), Buf("a2")] for _ in range(2)]
        B_THI = [[Buf("thi"), Buf("thi")] for _ in range(2)]
        B_HFS = [Buf("hf0"), Buf("hf1")]
        B_HMS1 = [B_HM, Buf("l1_hm1")]
        B_SQ = B_HM
        P.op("dve", ts(HBIAS[:].rearrange("p a r c -> p (a r c)"), PCOL[:, 2, 44:76], 0.5, None, ALU.mult),
             reads=[B_PCOL], writes=[B_L1C])
        P.op("dve", ts(CLH[:].rearrange("p r c -> p (r c)"), CLV[:, 0].rearrange("p r c -> p (r c)"), 0.5, None, ALU.mult),
             reads=[B_MISC], writes=[B_L1C])

        P.op("dve", cp(XE[:, :, 0:1], XT[:, :, 512:513]), reads=XBUF, writes=[B_XE])
        for b in range(2, 9):
            e0 = 1 + (b - 2) * 3
            t0 = (b + 1) * 256
            P.op("dve", cp(XE[:, :, e0:e0 + 3], XT[:, :, t0 - 2:t0 + 1]), reads=XBUF, writes=[B_XE])
        P.op("dve", cp(XE[:, :, 22:24], XT[:, :, NTOK - 2:NTOK]), reads=XBUF, writes=[B_XE])
        norm_mod(XE[:], [B_XE], 24, 1, 1, 1, HME, B_HME, HME, B_HME, RSTD, B_RSTD, TMPS, B_TMPS, 0)
        for c in range(8):
            P.pe([mm(psum[1][:, c * 24:(c + 1) * 24], W_IN1[:, k, c * 128:(c + 1) * 128], HME[:, k, :], k == 0, k == 7)
                  for k in range(8)], reads=[B_HME] + WIN1_B[0:2], writes=[PSB[1]])
        P.op("dve", cp(EDGE[:].rearrange("p c n -> p (c n)"), psum[1][:, 0:192]), reads=[PSB[1]], writes=[B_EDGE])
        B_SEND = Buf("send")
        P.op("dve", cp(SEND[:, :, 0:1], EDGE[:, :, 0:1]), reads=[B_EDGE], writes=[B_SEND])
        P.op("dve", cp(SEND[:, :, 1:3], EDGE[:, :, 22:24]), reads=[B_EDGE], writes=[B_SEND])
        B_CC2, B_CC2O = Buf("cc2"), Buf("cc2o")
        dq = P.new_dsem()
        P.dma("sp", lambda e: e.dma_start(out=cc2_in.ap(), in_=SEND.rearrange("p c n -> p (c n)")), dq, reads=[B_SEND], writes=[B_CC2])
        exchange(cc2_in, cc2_out, [B_CC2], B_CC2O)
        B_RC2 = Buf("rc2")
        dq = P.new_dsem()
        P.dma("sp", lambda e: e.dma_start(out=RC2.rearrange("p k c n -> p k (c n)"),
                                          in_=cc2_out.ap().rearrange("(k p) n -> p k n", p=128)), dq, reads=[B_CC2O], writes=[B_RC2])
        P.op("dve", ts(HALO[:, :, 0:2], RC2[:, 0, :, 1:3], FLG[:, 1:2], None, ALU.mult), reads=[B_RC2, B_MISC], writes=[B_HALO])
        P.op("dve", ts(HALO[:, :, 2:3], RC2[:, 1, :, 0:1], FLG[:, 0:1], None, ALU.mult), reads=[B_RC2, B_MISC], writes=[B_HALO])

        RINIT = lambda r: PCOL[:, 2, 92 + r * 8:92 + r * 8 + 8]

        def rg_norm(mb):
            hm, b_hm = HMS1[mb % 2], B_HMS1[mb % 2]
            norm_mod(xcols(mb * 256, 256), xbufs(mb * 256, 256), 256, 1, 1, cond_of_tile(2 * mb), hm, b_hm, hm, b_hm,
                     RSTD, B_RSTD, TMPS, B_TMPS, 0)

        def rg_block(mb, pass2):
            q = cond_of_tile(2 * mb)
            c0 = mb * 256
            HM, B_HM = HMS1[mb % 2], B_HMS1[mb % 2]
            for cpair in range(4):
                for z in range(2):
                    c = 2 * cpair + z
                    pb = 1 + z
                    P.pe([mm(psum[pb][:, 0:256], W_IN1[:, k, c * 128:(c + 1) * 128], HM[:, k, :], k == 0, k == 7) for k in range(8)],
                         reads=[B_HM, WIN1_B[c // 4]], writes=[PSB[pb]])
                    xbh, b_xbh = XBH[z], B_XBH[z]
                    P.op("act", act_fn(xbh[:, 2:258], psum[pb][:, 0:256], AF.Copy), reads=[PSB[pb]], writes=[b_xbh])
                    if mb < 2:
                        P.op("pool", lambda e, xbh=xbh: e.memset(xbh[:, 0:2], 0.0), writes=[b_xbh])
                        P.op("pool", lambda e, xbh=xbh: e.memset(xbh[:, 258:259], 0.0), writes=[b_xbh])
                    else:
                        if mb == 2:
                            P.op("pool", cp(xbh[:, 0:2], HALO[:, c, 0:2]), reads=[B_HALO], writes=[b_xbh])
                        else:
                            e0 = 1 + (mb - 3) * 3
                            P.op("pool", cp(xbh[:, 0:2], EDGE[:, c, e0:e0 + 2]), reads=[B_EDGE], writes=[b_xbh])
                        if mb == 9:
                            P.op("pool", cp(xbh[:, 258:259], HALO[:, c, 2:3]), reads=[B_HALO], writes=[b_xbh])
                        else:
                            e0 = 1 + (mb - 2) * 3 + 2
                            P.op("pool", cp(xbh[:, 258:259], EDGE[:, c, e0:e0 + 1]), reads=[B_EDGE], writes=[b_xbh])
                for j in range(4):
                    for z in range(2):
                        c = 2 * cpair + z
                        xbh, b_xbh, xc, b_xc = XBH[z], B_XBH[z], XC[z], B_XC[z]
                        if j == 0:
                            P.op("dve", ts(xc, xbh[:, 0:256], pcol(2, 4 + c), pcol(2, 36 + c), ALU.mult, ALU.add),
                                 reads=[b_xbh, B_PCOL], writes=[b_xc])
                        elif j < 3:
                            P.op("dve", stt(xc, xbh[:, j:j + 256], pcol(2, 4 + j * 8 + c), xc, ALU.mult, ALU.add),
                                 reads=[b_xbh, B_PCOL, b_xc], writes=[b_xc])
                        else:
                            P.op("dve", stt(XCB[:, c, :], xbh[:, 3:259], pcol(2, 4 + 24 + c), xc, ALU.mult, ALU.add),
                                 reads=[b_xbh, B_PCOL, b_xc], writes=[B_XCB])
            if pass2:
                for c2 in range(4):
                    for cc in range(2):
                        c = 2 * c2 + cc
                        P.pe([mm(psum[3][:, cc * 256:(cc + 1) * 256], W_IN1[:, k, 1024 + c * 128:1024 + (c + 1) * 128], HM[:, k, :],
                                 k == 0, k == 7) for k in range(8)], reads=[B_HM, WIN1_B[2 + c // 4]], writes=[PSB[3]])
                    P.op("act", act_fn(GG[:, 2 * c2:2 * c2 + 2, :], psum[3][:, 0:512].rearrange("p (c n) -> p c n", c=2),
                                       AF.Gelu_apprx_tanh), reads=[PSB[3]], writes=[B_GG])
            if mb + 1 < NMB:
                rg_norm(mb + 1)
            if mb < 2:
                P.op("dve", lambda e: e.memset(RUNF, 0.0), writes=[B_RUNF])
            if mb == 2:
                if pass2:
                    P.op("dve", cp(RUNF, RCV[:, 0:8]), reads=[B_RCV], writes=[B_RUNF])
                else:
                    P.op("dve", cp(RUNF, RINIT(0)), reads=[B_PCOL], writes=[B_RUNF])
            def grp_bufs(grp):
                sx = grp % 2
                return (2 * grp, THR[sx], A2[sx], THI[sx], HF[sx], B_THR[sx], B_A2[sx], B_THI[sx], B_HFS[sx], 4 + 2 * sx)

            def front(grp):
                ch0, thr, a2, thi, hf, b_thr, b_a2, b_thi, b_hf, pbank = grp_bufs(grp)
                for r in range(2):
                    for a_ in range(2):
                        fns = []
                        for ci in range(2):
                            c = ch0 + ci
                            fns.append(mm(psum[pbank + a_][:, ci * 256:(ci + 1) * 256], WGATE[:, a_, r, c, :], XCB[:, c, :], True, True))
                        P.pe(fns, reads=[B_XCB, B_WG], writes=[PSB[pbank + a_]])
                    for ci in range(2):
                        c = ch0 + ci
                        P.op("act", act_fn(thr[r][:, ci, :], psum[pbank][:, ci * 256:(ci + 1) * 256], AF.Tanh, scale=0.5,
                                           bias=HBIAS[:, 0, r, c:c + 1]), reads=[PSB[pbank], B_L1C], writes=[b_thr[r]])
                    for ci in range(2):
                        c = ch0 + ci
                        P.op("act", act_fn(thi[r][:, ci, :], psum[pbank + 1][:, ci * 256:(ci + 1) * 256], AF.Tanh, scale=0.5,
                                           bias=HBIAS[:, 1, r, c:c + 1]), reads=[PSB[pbank + 1], B_L1C], writes=[b_thi[r]])
                    P.op("dve", stt(thr[r][:], thr[r][:], 1.0, CLH[:, r, ch0:ch0 + 2].unsqueeze(2).to_broadcast([128, 2, 256]),
                                    ALU.add, ALU.mult), reads=[b_thr[r], B_L1C], writes=[b_thr[r]])
                    if (not pass2) and r == 1:
                        rs = slice(2 * (grp % 2), 2 * (grp % 2) + 2)
                        P.op("dve", lambda e, thr=thr, rs=rs: e.reduce_sum(out=RACC[:, rs], in_=thr[1][:], axis=mybir.AxisListType.X),
                             reads=[b_thr[1]], writes=[B_RACC])
                        P.op("act", act_fn(PBLK[:, mb, ch0:ch0 + 2], RACC[:, rs], AF.Exp), reads=[B_RACC], writes=[B_BLK])
                    P.op("act", act_fn(thr[r][:], thr[r][:], AF.Exp), reads=[b_thr[r]], writes=[b_thr[r]])
                    P.op("dve", tt(a2[r][:], thr[r][:], thr[r][:], ALU.mult), reads=[b_thr[r]], writes=[b_a2[r]])

            def back(grp):
                ch0, thr, a2, thi, hf, b_thr, b_a2, b_thi, b_hf, pbank = grp_bufs(grp)
                for r in range(2):
                    P.op("act", act_fn(a2[r][:], a2[r][:], AF.Sqrt, bias=1.0, scale=-1.0), reads=[b_a2[r]], writes=[b_a2[r]])
                for r in range(2):
                    P.op("dve", stt(thi[r][:], thi[r][:], 1.0, a2[r][:], ALU.add, ALU.mult), reads=[b_thi[r], b_a2[r]], writes=[b_thi[r]])
                for r in range(2):
                    P.op("dve", stt(thi[r][:], thi[r][:], 0.5, XCB[:, ch0:ch0 + 2, :], ALU.mult, ALU.mult),
                         reads=[b_thi[r], B_XCB], writes=[b_thi[r]])
                hbv = a2[0]
                for ci in range(2):
                    c = ch0 + ci
                    P.op("dve", lambda e, ci=ci, c=c: e.tensor_tensor_scan(
                        out=hf[:, ci, :], data0=thr[0][:, ci, :], data1=thi[0][:, ci, :], initial=RUNF[:, c:c + 1],
                        op0=ALU.mult, op1=ALU.add), reads=[b_thr[0], b_thi[0], B_RUNF], writes=[b_hf])
                    init = SINB[:, mb, c:c + 1] if pass2 else 0.0
                    P.op("dve", lambda e, ci=ci, init=init: e.tensor_tensor_scan(
                        out=hbv[:, ci, ::-1], data0=thr[1][:, ci, ::-1], data1=thi[1][:, ci, ::-1], initial=init,
                        op0=ALU.mult, op1=ALU.add), reads=[b_thr[1], b_thi[1], B_BLK], writes=[b_a2[0]])
                P.op("dve", cp(RUNF[:, ch0:ch0 + 2], hf[:, :, 255]), reads=[b_hf], writes=[B_RUNF])
                if (not pass2) or mb < 2:
                    P.op("dve", cp(FBLK[:, mb, ch0:ch0 + 2], hbv[:, :, 0]), reads=[b_a2[0]], writes=[B_BLK])
                if pass2:
                    P.op("dve", tt(hf[:], hf[:], hbv[:], ALU.add), reads=[b_hf, b_a2[0]], writes=[b_hf])
                    P.op("dve", tt(CAT[:, ch0:ch0 + 2, :], hf[:], GG[:, ch0:ch0 + 2, :], ALU.mult), reads=[b_hf, B_GG], writes=[B_CAT])

            front(0)
            for grp in range(4):
                if grp + 1 < 4:
                    front(grp + 1)
                back(grp)

        for mb in range(2):
            P.op("dve", lambda e, mb=mb: e.memset(SINB[:, mb, :], 0.0), writes=[B_BLK])
        rg_norm(2)
        for mb in range(2, NMB):
            rg_block(mb, False)
        P.op("dve", cp(RUNB, RINIT(1)), reads=[B_PCOL], writes=[B_RUNB])
        P.op("dve", lambda e: e.memset(RUNP, 1.0), writes=[B_RUNP])
        for mb in range(9, 1, -1):
            P.op("dve", cp(SINB[:, mb, :], RUNB), reads=[B_RUNB], writes=[B_BLK])
            P.op("dve", cp(CPB[:, mb, :], RUNP), reads=[B_RUNP], writes=[B_BLK])
            P.op("dve", tt(RUNB, RUNB, PBLK[:, mb, :], ALU.mult), reads=[B_RUNB, B_BLK], writes=[B_RUNB])
            P.op("dve", tt(RUNB, RUNB, FBLK[:, mb, :], ALU.add), reads=[B_RUNB, B_BLK], writes=[B_RUNB])
            P.op("dve", tt(RUNP, RUNP, PBLK[:, mb, :], ALU.mult), reads=[B_RUNP, B_BLK], writes=[B_RUNP])
        B_SEND3 = Buf("send3")
        P.op("dve", cp(SEND3[:, 0:8], RUNF), reads=[B_RUNF], writes=[B_SEND3])
        P.op("dve", cp(SEND3[:, 8:16], RUNB), reads=[B_RUNB], writes=[B_SEND3])
        B_CC3, B_CC3O = Buf("cc3"), Buf("cc3o")
        dq = P.new_dsem()
        P.dma("sp", lambda e: e.dma_start(out=cc3_in.ap(), in_=SEND3), dq, reads=[B_SEND3], writes=[B_CC3])
        exchange(cc3_in, cc3_out, [B_CC3], B_CC3O)
        B_RC3 = Buf("rc3")
        dq = P.new_dsem()
        P.dma("sp", lambda e: e.dma_start(out=RC3, in_=cc3_out.ap().rearrange("(k p) n -> p k n", p=128)), dq,
              reads=[B_CC3O], writes=[B_RC3])
        P.op("dve", stt(RCV[:, 0:8], RC3[:, 0, 0:8], FLG[:, 1:2], RINIT(0), ALU.mult, ALU.add), reads=[B_RC3, B_MISC, B_PCOL], writes=[B_RCV])
        P.op("dve", ts(RCV[:, 8:16], RC3[:, 1, 8:16], FLG[:, 0:1], None, ALU.mult), reads=[B_RC3, B_MISC], writes=[B_RCV])
        for mb in range(2, 10):
            P.op("dve", tt(CPB[:, mb, :], CPB[:, mb, :], RCV[:, 8:16], ALU.mult), reads=[B_BLK, B_RCV], writes=[B_BLK])
            P.op("dve", tt(SINB[:, mb, :], SINB[:, mb, :], CPB[:, mb, :], ALU.add), reads=[B_BLK], writes=[B_BLK])
        WS1 = WStream(WOUT1_OFF, rg_w_out)
        WS1.prefetch()

        def out_proj_and_update1(mb, q):
            WS1.project(CAT, B_CAT, 256, lambda oc: psum[4 + oc // 2][:, (oc % 2) * 256:(oc % 2) * 256 + 256],
                        lambda oc: PSB[4 + oc // 2], mb == NMB - 1)
            ych = [psum[4 + c // 2][:, (c % 2) * 256:(c % 2) * 256 + 256] for c in range(8)]
            post_update(ych, PSB[4:8], mb * 256, 256, 1, 1, q, SQ, B_SQ, RSTD, B_RSTD, TMPS, B_TMPS, 0,
                        y_all=PS[:, 2048:4096].rearrange("p (c n) -> p c n", c=8))

        rg_norm(0)
        for mb in range(NMB):
            SQ, B_SQ = HMS1[mb % 2], B_HMS1[mb % 2]
            rg_block(mb, True)
            if mb < 2:
                P.op("dve", cp(NSR[:, mb, 0, :], RUNF), reads=[B_RUNF], writes=[B_NSR])
                P.op("dve", cp(NSR[:, mb, 1, :], FBLK[:, mb, :]), reads=[B_BLK], writes=[B_NSR])
            out_proj_and_update1(mb, cond_of_tile(2 * mb))
            if mb == 1:
                P.pe([lambda e: e.transpose(psum[3][0:32, 0:128], NSR.rearrange("p s r c -> p (s r c)"), IDENT)],
                     reads=[B_NSR, B_CST], writes=[PSB[3]])
                B_NSRT = Buf("nsrt")
                P.op("dve", cp(NSRT[0:32, :], psum[3][0:32, 0:128]), reads=[PSB[3]], writes=[B_NSRT])
                dq = P.new_dsem()
                P.dma("sp", lambda e: e.dma_start(out=nsr_d.ap().rearrange("s r (c p) -> (s r c) p", p=128), in_=NSRT[0:32, :]), dq,
                      reads=[B_NSRT])


        P.barrier()
        dump_x("l1mix")

        mlp_layer(1, True)
        P.barrier()
        dump_x("final")
        P.barrier()

        with nc.Block() as block:
            @block.tensor
            def _(e):
                for f in P.st["pe"].ops:
                    f(e)

            @block.scalar
            def _(e):
                for f in P.st["act"].ops:
                    f(e)

            @block.vector
            def _(e):
                for f in P.st["dve"].ops:
                    f(e)

            @block.gpsimd
            def _(e):
                for f in P.st["pool"].ops:
                    f(e)

            @block.sync
            def _(e):
                for f in P.st["sp"].ops:
                    f(e)
    return nc, P.nops


def _prow(cidx, b, inp):
    g0 = np.zeros((128, 128), np.float32)
    g0[0:96] = inp["mod_b"].reshape(96, 128)
    g0[96:112] = inp["mlp_b2"].reshape(16, 128)
    g0[112:120] = inp["c_ctx"].reshape(8, 128)
    g0[120:128] = inp["c"][b].reshape(8, 128)
    g1 = np.zeros((128, 128), np.float32)
    g1[0:64] = inp["norm_g"].reshape(64, 128)
    g1[64:128] = inp["mlp_b1"].reshape(64, 128)
    g2 = np.zeros((128, 128), np.float32)
    g2[0:4] = inp["gla_norm_g"].reshape(4, 128)
    g2[4:36] = inp["rg_conv_w"].reshape(32, 128)
    g2[36:44] = inp["rg_conv_b"].reshape(8, 128)
    g2[44:60] = inp["rg_ba"].reshape(16, 128)
    g2[60:76] = inp["rg_bx"].reshape(16, 128)
    g2[76:92] = inp["rg_L"].reshape(16, 128)
    isA = (cidx % 2 == 0)
    st = inp["state_rglru"][b, 0]
    g2[92:100] = st[0].reshape(8, 128) if isA else 0.0
    g2[100:108] = 0.0 if isA else st[1].reshape(8, 128)
    return np.stack([g0, g1, g2])


_CACHE = {}


def kernel(**inputs):
    import os
    inp = {k: np.ascontiguousarray(np.asarray(v)) for k, v in inputs.items()}
    dbg = os.environ.get("KDBG")
    stop = os.environ.get("KSTOP")
    key = (dbg, stop)
    if key not in _CACHE:
        _CACHE[key] = build_program(dbg, stop)
    nc, nops = _CACHE[key]
    shared = {
        "mod_w": inp["mod_w"], "mlp_w1": inp["mlp_w1"], "mlp_w2": inp["mlp_w2"],
        "ev_w_in": inp["ev_w_in"][0], "ev_w_out": inp["ev_w_out"][0],
        "sgu_ln_g": inp["sgu_ln_g"].reshape(1, 512), "sgu_ln_b": inp["sgu_ln_b"].reshape(1, 512),
        "sgu_ws": inp["sgu_ws"][0], "sgu_bs": inp["sgu_bs"].reshape(1, 512),
        "gw2": inp["gla_gate_w2"][0], "gb": inp["gla_gate_b"].reshape(1, 512),
        "rgb": np.concatenate([inp["rg_ba"].reshape(-1), inp["rg_bx"].reshape(-1)]).reshape(1, 4096),
        "rg_w_in": inp["rg_w_in"][0], "rg_wa": inp["rg_wa"][0], "rg_wx": inp["rg_wx"][0], "rg_w_out": inp["rg_w_out"][0],
    }
    in_maps = []
    for cidx in range(NCORE):
        b = cidx // 2
        isA = (cidx % 2 == 0)
        half = cidx % 2
        xt = np.concatenate([inp["x_prompt"][2 * cidx].reshape(256, D), inp["x_prompt"][2 * cidx + 1].reshape(256, D),
                             inp["x_sample"][b, half * 2048:(half + 1) * 2048]], axis=0)
        flags = np.zeros((128, 4), np.float32)
        flags[:, 0] = 1.0 if isA else 0.0
        flags[:, 1] = 0.0 if isA else 1.0
        flags[:, 2] = 0.0 if isA else 32.0
        gi = np.zeros((2, 4, 64, 128), np.float32)
        if isA:
            gi[0] = inp["state_gla"][b, 0, 0]
        else:
            gi[1] = inp["state_gla"][b, 0, 1]
        m = dict(shared)
        m.update({"xtok": np.ascontiguousarray(xt), "prow": _prow(cidx, b, inp), "flags": flags, "ginit": gi})
        in_maps.append(m)
    res = run_bass_kernel_spmd(nc, in_maps, core_ids=list(range(NCORE)))
    R = res.results
    y_prompt = np.zeros((16, 256, D), np.float32)
    y_sample = np.zeros((4, 4096, D), np.float32)
    nsg = np.zeros((16, 1, 2, 4, 64, 128), np.float32)
    nsr = np.zeros((16, 1, 2, D), np.float32)
    for cidx in range(NCORE):
        r = R[cidx]
        b = cidx // 2
        half = cidx % 2
        y = r["y"]
        y_prompt[2 * cidx] = y[0:256]
        y_prompt[2 * cidx + 1] = y[256:512]
        y_sample[b, half * 2048:(half + 1) * 2048] = y[512:]
        nsg[2 * cidx:2 * cidx + 2, 0] = r["nsg"]
        nsr[2 * cidx:2 * cidx + 2, 0] = r["nsr"]
    if dbg:
        kernel.dbg = [R[c]["dbg"] for c in range(NCORE)]
    return (y_prompt, y_sample, nsg, nsr)
```
